# Optimizing a Trainium2 kernel written in Bass

```python
import math
import jax, jax.numpy as jnp
from jax import lax
import numpy as np

D_MODEL = 2048
BATCH = 8
SEQ = 2048
DEPTH = 1

MIX_WIDTH = D_MODEL
POOL_WIDTH = D_MODEL // 2
POOL_WINDOWS = (2, 4, 8, 16)
POOL_GROUP = POOL_WIDTH // len(POOL_WINDOWS)
RET_WIDTH = MIX_WIDTH - POOL_WIDTH
RET_HEADS = 4
RET_HEAD_DIM = RET_WIDTH // RET_HEADS
RET_CHUNK = 128
ROPE_BASE = 10000.0
D_FF = -(-8 * D_MODEL // (3 * 256)) * 256
IN_WIDTH = POOL_WIDTH + 4 * RET_WIDTH
N_MOD = 6
EPS = 1e-6

kernel_name = 'hybrid_pool_retention_encoder_block'


def rmsnorm(x, g):
    xf = x.astype(jnp.float32)
    y = xf * lax.rsqrt(jnp.mean(xf * xf, axis=-1, keepdims=True) + EPS)
    return (y * g.astype(jnp.float32)).astype(x.dtype)


def rotary(t):
    S, dh = t.shape[2], t.shape[3]
    half = dh // 2
    inv = 1.0 / (ROPE_BASE ** jnp.linspace(0.0, 1.0, half, dtype=jnp.float32))
    ang = jnp.arange(S, dtype=jnp.float32)[:, None] * inv[None, :]
    cos, sin = jnp.cos(ang), jnp.sin(ang)
    t1, t2 = t[..., :half], t[..., half:]
    return jnp.concatenate([t1 * cos - t2 * sin, t1 * sin + t2 * cos], axis=-1)


def multi_scale_pool(u):
    S = u.shape[1]
    t = jnp.arange(S)
    outs = []
    for gi, w in enumerate(POOL_WINDOWS):
        ug = u[..., gi * POOL_GROUP:(gi + 1) * POOL_GROUP].astype(jnp.float32)
        cs = jnp.concatenate([jnp.zeros_like(ug[:, :1]), jnp.cumsum(ug, axis=1)], axis=1)
        lo = jnp.clip(t - w // 2, 0, S)
        hi = jnp.clip(t + w // 2, 0, S)
        total = cs[:, hi] - cs[:, lo]
        count = (hi - lo).astype(jnp.float32)[None, :, None]
        outs.append(total / count - ug)
    return jnp.concatenate(outs, axis=-1)


def retention_one_direction(q, k, v, log_gamma, include_diag):
    B, H, S, dh = q.shape
    C = RET_CHUNK
    n = S // C
    qc = q.reshape(B, H, n, C, dh)
    kc = k.reshape(B, H, n, C, dh)
    vc = v.reshape(B, H, n, C, dh)
    i = jnp.arange(C)
    diff = (i[:, None] - i[None, :]).astype(jnp.float32)
    mask = diff >= 0 if include_diag else diff > 0
    lg = log_gamma[:, None, None]
    dmat = jnp.where(mask, jnp.exp(jnp.where(mask, diff, 0.0) * lg), 0.0)
    scores = jnp.einsum('bhnid,bhnjd->bhnij', qc, kc) * dmat[None, :, None]
    inner = jnp.einsum('bhnij,bhnjd->bhnid', scores, vc)
    k_decay = jnp.exp((C - 1 - i).astype(jnp.float32)[None, :] * log_gamma[:, None])
    q_decay = jnp.exp((i + 1).astype(jnp.float32)[None, :] * log_gamma[:, None])
    chunk_decay = jnp.exp(C * log_gamma)

    def step(state, qkv):
        q_, k_, v_ = qkv
        cross = jnp.einsum('bhid,bhde->bhie', q_, state) * q_decay[None, :, :, None]
        state = state * chunk_decay[None, :, None, None] + jnp.einsum(
            'bhjd,bhje->bhde', k_ * k_decay[None, :, :, None], v_)
        return state, cross

    xs = (jnp.moveaxis(qc, 2, 0), jnp.moveaxis(kc, 2, 0), jnp.moveaxis(vc, 2, 0))
    state0 = jnp.zeros((B, H, dh, dh), jnp.float32)
    _, cross = lax.scan(step, state0, xs)
    cross = jnp.moveaxis(cross, 0, 2)
    return (inner + cross).reshape(B, H, S, dh)


def setup_inputs(seed: int = 0) -> dict:
    key = jax.random.key(seed)
    ks = jax.random.split(key, 20)
    f32 = jnp.float32
    L = DEPTH

    def nrm(k, shape, fan_in):
        return jax.random.normal(k, shape, f32) * (fan_in ** -0.5)

    base = np.log(-np.log(1.0 - 2.0 ** (-5.0 - np.arange(RET_HEADS)))).astype(np.float32)
    return {
        'x': jax.random.normal(ks[0], (BATCH, SEQ, D_MODEL), f32),
        'c': jax.random.normal(ks[1], (BATCH, D_MODEL), f32),
        'w_ada': nrm(ks[2], (L, D_MODEL, N_MOD * D_MODEL), D_MODEL),
        'b_ada': 0.02 * jax.random.normal(ks[3], (L, N_MOD * D_MODEL), f32),
        'norm1_g': 1.0 + 0.02 * jax.random.normal(ks[4], (L, D_MODEL), f32),
        'w_in': nrm(ks[5], (L, D_MODEL, IN_WIDTH), D_MODEL),
        'pool_w': nrm(ks[6], (L, len(POOL_WINDOWS), POOL_GROUP, POOL_GROUP), POOL_GROUP),
        'pool_scale': 1.0 + 0.02 * jax.random.normal(ks[7], (L, POOL_WIDTH), f32),
        'ret_decay_fwd': jnp.asarray(base)[None, :] + 0.05 * jax.random.normal(ks[8], (L, RET_HEADS), f32),
        'ret_decay_bwd': jnp.asarray(base)[None, :] + 0.05 * jax.random.normal(ks[9], (L, RET_HEADS), f32),
        'w_out': nrm(ks[10], (L, MIX_WIDTH, D_MODEL), MIX_WIDTH),
        'norm2_g': 1.0 + 0.02 * jax.random.normal(ks[11], (L, D_MODEL), f32),
        'w_gate': nrm(ks[12], (L, D_MODEL, D_FF), D_MODEL),
        'w_up': nrm(ks[13], (L, D_MODEL, D_FF), D_MODEL),
        'w_down': nrm(ks[14], (L, D_FF, D_MODEL), D_FF),
        'final_g': 1.0 + 0.02 * jax.random.normal(ks[15], (D_MODEL,), f32),
    }


def reference(x, c, w_ada, b_ada, norm1_g, w_in, pool_w, pool_scale, ret_decay_fwd,
              ret_decay_bwd, w_out, norm2_g, w_gate, w_up, w_down, final_g):
    B, S, D = x.shape
    H, dh = RET_HEADS, RET_HEAD_DIM
    c_act = jax.nn.silu(c)

    def to_heads(t):
        return t.reshape(B, S, H, dh).transpose(0, 2, 1, 3).astype(jnp.float32)

    for l in range(DEPTH):
        mod = (c_act @ w_ada[l] + b_ada[l])[:, None, :]
        sh1, sc1, g1, sh2, sc2, g2 = jnp.split(mod, N_MOD, axis=-1)

        h = rmsnorm(x, norm1_g[l]) * (1.0 + sc1) + sh1
        proj = h @ w_in[l]
        o = POOL_WIDTH
        u_pool = proj[..., :o]
        q = proj[..., o:o + RET_WIDTH]
        k = proj[..., o + RET_WIDTH:o + 2 * RET_WIDTH]
        v = proj[..., o + 2 * RET_WIDTH:o + 3 * RET_WIDTH]
        g = proj[..., o + 3 * RET_WIDTH:]

        pooled = multi_scale_pool(u_pool).reshape(B, S, len(POOL_WINDOWS), POOL_GROUP)
        pool_out = jnp.einsum('bsgc,gcd->bsgd', pooled, pool_w[l].astype(jnp.float32))
        pool_out = (pool_out.reshape(B, S, POOL_WIDTH) * pool_scale[l]).astype(x.dtype)

        qh = rotary(to_heads(q)) * (dh ** -0.5)
        kh = rotary(to_heads(k))
        vh = to_heads(v)
        lg_f = -jnp.exp(ret_decay_fwd[l].astype(jnp.float32))
        lg_b = -jnp.exp(ret_decay_bwd[l].astype(jnp.float32))
        ret_f = retention_one_direction(qh, kh, vh, lg_f, True)
        ret_b = jnp.flip(retention_one_direction(
            jnp.flip(qh, 2), jnp.flip(kh, 2), jnp.flip(vh, 2), lg_b, False), 2)
        ret = ret_f + ret_b
        ret = ret * lax.rsqrt(jnp.mean(ret * ret, axis=-1, keepdims=True) + EPS)
        ret = ret.transpose(0, 2, 1, 3).reshape(B, S, RET_WIDTH).astype(x.dtype)
        ret_out = jax.nn.silu(g) * ret

        mixed = jnp.concatenate([pool_out, ret_out], axis=-1) @ w_out[l]
        x = x + g1 * mixed

        h2 = rmsnorm(x, norm2_g[l]) * (1.0 + sc2) + sh2
        ffn = (jax.nn.silu(h2 @ w_gate[l]) * (h2 @ w_up[l])) @ w_down[l]
        x = x + g2 * ffn

    return rmsnorm(x, final_g)
```

```python
import numpy as np
import concourse.bass as bass
import concourse.mybir as mybir
from concourse.bass_utils import run_bass_kernel_spmd

F32 = mybir.dt.float32
BF16 = mybir.dt.bfloat16
U8 = mybir.dt.uint8
ALU = mybir.AluOpType
AF = mybir.ActivationFunctionType

S = 2048
D = 2048
KC = 16
DFF = 5632
NFC = 44
INW = 5120
CH = 128
NCH = 16
NH = 4
DH = 256
EPS = 1e-6
NCORES = 8


class Buf:
    def __init__(self, name):
        self.name = name
        self.w = None
        self.r = {}


class DSem:
    def __init__(self, nc, name):
        self.h = nc.alloc_semaphore(name)
        self.val = 0


class Sched:
    ENG = ("pe", "act", "dve", "pool", "sp")

    def __init__(self, nc):
        self.nc = nc
        self.sem = {e: nc.alloc_semaphore("eng_" + e) for e in self.ENG}
        self.cnt = {e: 0 for e in self.ENG}
        self.seen = {e: {} for e in self.ENG}
        self.prog = {e: [] for e in self.ENG}
        self.dsems = []

    def dsem(self, name):
        d = DSem(self.nc, name)
        self.dsems.append(d)
        return d

    def _deps(self, e, reads, writes):
        need = {}

        def add(tok):
            if tok is None:
                return
            s, v = tok
            k = id(s)
            if k not in need or need[k][1] < v:
                need[k] = (s, v)

        for b in reads:
            add(b.w)
        for b in writes:
            add(b.w)
            for t in b.r.values():
                add(t)
        out = []
        for k, (s, v) in need.items():
            if e == "pe" and s is self.sem["pe"]:
                continue
            if self.seen[e].get(k, 0) >= v:
                continue
            self.seen[e][k] = v
            out.append((s, v))
        return out

    def _mark(self, tok, reads, writes):
        for b in reads:
            b.r[id(tok[0])] = tok
        for b in writes:
            b.w = tok
            b.r = {}

    def op(self, e, fn, reads=(), writes=()):
        waits = self._deps(e, reads, writes)
        self.cnt[e] += 1
        tok = (self.sem[e], self.cnt[e])

        def emit(eng, waits=waits, fn=fn, s=tok[0]):
            for ws, wv in waits:
                eng.wait_ge(ws, wv)
            fn(eng).then_inc(s, 1)

        self.prog[e].append(emit)
        self._mark(tok, reads, writes)
        return tok

    def dma(self, q, dsem, out_ap, in_ap, reads=(), writes=(), **kw):
        waits = self._deps(q, reads, writes)
        dsem.val += 16
        tok = (dsem.h, dsem.val)

        def emit(eng, waits=waits, s=dsem.h):
            for ws, wv in waits:
                eng.wait_ge(ws, wv)
            eng.dma_start(out=out_ap, in_=in_ap, **kw).then_inc(s, 16)

        self.prog[q].append(emit)
        self._mark(tok, reads, writes)
        return tok

    def barrier(self, skip=()):
        toks = [(self.sem[e], self.cnt[e]) for e in self.ENG if self.cnt[e] > 0]
        toks += [(d.h, d.val) for d in self.dsems if d.val > 0]
        for e in self.ENG:
            if e in skip:
                continue
            waits = []
            for s, v in toks:
                k = id(s)
                if self.seen[e].get(k, 0) >= v:
                    continue
                if e == "pe" and s is self.sem["pe"]:
                    continue
                self.seen[e][k] = v
                waits.append((s, v))

            def emit(eng, waits=waits):
                for ws, wv in waits:
                    eng.wait_ge(ws, wv)

            self.prog[e].append(emit)

    def replay(self):
        nc = self.nc
        with nc.Block() as blk:
            @blk.tensor
            def _(e):
                for f in self.prog["pe"]:
                    f(e)

            @blk.scalar
            def _(e):
                for f in self.prog["act"]:
                    f(e)

            @blk.vector
            def _(e):
                for f in self.prog["dve"]:
                    f(e)

            @blk.gpsimd
            def _(e):
                for f in self.prog["pool"]:
                    f(e)

            @blk.sync
            def _(e):
                for f in self.prog["sp"]:
                    f(e)


class Arena:
    def __init__(self, nc, nbytes):
        self.t = nc.alloc_sbuf_tensor("arena", [128, nbytes], U8)
        self.n = nbytes
        self.off = 0

    def mark(self):
        return self.off

    def reset(self, m):
        self.off = m

    def alloc(self, shape, dtype, parts=128):
        esz = 2 if dtype == BF16 else 4
        n = int(np.prod(shape)) * esz
        n = (n + 63) // 64 * 64
        assert self.off + n <= self.n, f"SBUF arena overflow {self.off}+{n}>{self.n}"
        ap = self.t[0:parts, self.off:self.off + n]
        self.off += n
        ap = ap.bitcast(dtype)
        ne = int(np.prod(shape))
        ap = ap[:, 0:ne]
        if len(shape) == 2:
            ap = ap.rearrange("p (a b) -> p a b", b=shape[1])
        elif len(shape) == 3:
            ap = ap.rearrange("p (a b c) -> p a b c", b=shape[1], c=shape[2])
        return ap


def run_pipeline(gens, window):
    for _ in pipeline_rounds(gens, window):
        pass


def pipeline_rounds(gens, window):
    active = []
    it = iter(gens)
    pending = True
    while pending or active:
        if pending and len(active) < window:
            try:
                active.append(next(it))
            except StopIteration:
                pending = False
        for g in reversed(list(active)):
            try:
                next(g)
            except StopIteration:
                active.remove(g)
        yield


LN16 = -2.772588722239781


def build_nc(debug_stop=None):
    nc = bass.Bass("TRN2", target_bir_lowering=False)
    dbg_kind = "ExternalOutput" if debug_stop else "Internal"

    def din(name, shape, dt=F32):
        return nc.dram_tensor(name, shape, dt, kind="ExternalInput").ap()

    x = din("x", [S, D])
    cT = din("cT", [128, KC])
    w_ada = din("w_ada", [D, 6 * D])
    b_ada = din("b_ada", [1, 6 * D])
    g1n = din("g1n", [128, KC])
    g2n = din("g2n", [128, KC])
    fgr = din("fg", [D])
    g1row = din("g1row", [D])
    g2row = din("g2row", [D])
    w_in = din("w_in", [D, INW])
    pool_w = din("pool_w", [4, 256, 256])
    pscale = din("pscale", [128, 8])
    decay = din("decay", [128, 8])
    w_out = din("w_out", [D, D])
    w_gate = din("w_gate", [D, DFF])
    w_up = din("w_up", [D, DFF])
    w_down = din("w_down", [DFF, D])
    cs = din("cs", [128, 2, S])
    cmat = din("cmat", [128, 514])
    invc = din("invc", [4, 128, S])
    identf = din("identf", [128, 128])
    out = nc.dram_tensor("out", [S, D], F32, kind="ExternalOutput").ap()
    catT = nc.dram_tensor("catT", [16, 128, S], BF16, kind=dbg_kind).ap()
    modrows = nc.dram_tensor("modrows", [6, D], F32, kind=dbg_kind).ap()

    sc = Sched(nc)
    ar = Arena(nc, 207 * 1024)
    PS = [nc.alloc_psum_tensor(f"ps{i}", [128, 512], F32) for i in range(8)]
    PB = [Buf(f"psb{i}") for i in range(8)]

    NSC = 8 + 22 + 44
    wsc = nc.dram_tensor("wsc", [NSC, 128, 4096], BF16).ap()
    B_wsc = [Buf(f"wsc{i}") for i in range(NSC)]
    d_pre = sc.dsem("dprecast")
    pre_jobs = []
    for db in range(4):
        for hf in range(2):
            pre_jobs.append((db * 2 + hf, wsc[db * 2 + hf][:, 0:4096].rearrange("p (kc n) -> p kc n", n=512),
                             w_out[hf * 1024:(hf + 1) * 1024, db * 512:(db + 1) * 512].rearrange("(kc p) n -> p kc n", p=128)))
    for db in range(4):
        for jg in range(11):
            sid = 30 + db * 11 + jg
            pre_jobs.append((sid, wsc[sid][:, 0:2048].rearrange("p (a n) -> p a n", n=512),
                             w_down[jg * 512:(jg + 1) * 512, db * 512:(db + 1) * 512].rearrange("(a p) n -> p a n", p=128)))
    pre_it = iter(pre_jobs)

    def precast(n):
        for _ in range(n):
            job = next(pre_it, None)
            if job is None:
                return
            sid, dst, src = job
            sc.dma("pool", d_pre, dst, src, writes=[B_wsc[sid]])

    ident = ar.alloc((128,), BF16)
    B_ident = Buf("ident")
    epsb = ar.alloc((1,), F32)
    a1 = ar.alloc((KC,), F32)
    b1 = ar.alloc((KC,), F32)
    a2 = ar.alloc((KC,), F32)
    b2 = ar.alloc((KC,), F32)
    g1n_sb = ar.alloc((KC,), F32)
    g2n_sb = ar.alloc((KC,), F32)
    psc_sb = ar.alloc((8,), F32)
    cact = ar.alloc((KC,), BF16)
    sc2_pp = ar.alloc((KC,), F32)
    B_small = Buf("small")
    B_a1, B_a2 = Buf("a1"), Buf("a2")
    B_cact = Buf("cact")
    B_mod = [Buf(f"modrow{v}") for v in range(6)]
    B_cat = [Buf(f"cat{i}") for i in range(16)]
    d_mod = [sc.dsem(f"dmod{v}") for v in range(6)]
    d_misc = sc.dsem("dmisc")
    d_misc2 = sc.dsem("dmisc2")

    NSLOT = 3
    wslot = [ar.alloc((4096,), BF16) for _ in range(NSLOT)]
    B_ws = [Buf(f"ws{i}") for i in range(NSLOT)]
    d_ws = [sc.dsem(f"dws{i}") for i in range(NSLOT)]
    slot_ctr = [0]

    def load_w(src_ap, ncols_total, view):
        i = slot_ctr[0] % len(wslot)
        slot_ctr[0] += 1
        dst = wslot[i][:, 0:ncols_total]
        dst = view(dst)
        sc.dma("pool", d_ws[i], dst, src_ap, writes=[B_ws[i]])
        return dst, B_ws[i]

    def v3(b):
        return lambda ap: ap.rearrange("p (a b) -> p a b", b=b)

    tmpf = ar.alloc((128,), F32)
    cT_sb = ar.alloc((KC,), F32)
    sc.dma("sp", d_misc, tmpf, identf, writes=[B_ident])
    sc.dma("sp", d_misc, cT_sb, cT, writes=[B_ident])
    sc.dma("sp", d_misc, g1n_sb, g1n, writes=[B_ident])
    sc.dma("sp", d_misc, g2n_sb, g2n, writes=[B_ident])
    sc.dma("sp", d_misc, psc_sb, pscale, writes=[B_ident])
    sc.op("dve", lambda e: e.tensor_copy(out=ident, in_=tmpf), reads=[B_ident], writes=[B_small])
    sc.op("dve", lambda e: e.memset(epsb, EPS), writes=[B_small])
    sc.op("act", lambda e: e.activation(out=cact, in_=cT_sb, func=AF.Silu), reads=[B_ident], writes=[B_cact])

    brow = [ar.alloc((256,), F32, parts=1) for _ in range(2)]
    B_brow = [Buf("brow0"), Buf("brow1")]
    d_brow = [sc.dsem("dbrow0"), sc.dsem("dbrow1")]
    mrow = [ar.alloc((256,), F32, parts=1) for _ in range(2)]
    B_mrow = [Buf("mrow0"), Buf("mrow1")]
    mctr = [0]

    def mod_gen(v, bank):
        for cb in range(8):
            mod_slab(v, (bank, 13 - bank)[cb % 2], cb)
            yield

    def mod_vector(v, bank):
        for cb in range(8):
            mod_slab(v, bank, cb)

    def mod_slab(v, bank, cb):
        if True:
            c0 = v * D + cb * 256
            slab, bs = load_w(w_ada[:, c0:c0 + 256].rearrange("(kc p) n -> p kc n", p=128), 4096, v3(256))
            if v >= 2:
                precast(1)
            j = mctr[0] % 2
            mctr[0] += 1
            sc.dma("sp", d_brow[j], brow[j], b_ada[0:1, c0:c0 + 256], writes=[B_brow[j]])

            def fn(e, slab=slab):
                ins = None
                for kc in range(KC):
                    ins = e.matmul(PS[bank][0:1, 0:256], lhsT=cact[:, kc:kc + 1], rhs=slab[:, kc, :],
                                   start=(kc == 0), stop=(kc == KC - 1))
                return ins
            sc.op("pe", fn, reads=[bs, B_cact], writes=[PB[bank]])
            sc.op("dve", lambda e, j=j: e.tensor_tensor(out=mrow[j], in0=PS[bank][0:1, 0:256], in1=brow[j], op=ALU.add),
                  reads=[B_brow[j]], writes=[PB[bank], B_mrow[j]])
            sc.dma("sp", d_mod[v], modrows[v:v + 1, cb * 256:(cb + 1) * 256], mrow[j], reads=[B_mrow[j]], writes=[B_mod[v]])

    def load_pp(dst, v, buf):
        sc.dma("sp", d_misc2, dst, modrows[v].rearrange("(kc p) -> p kc", p=128), reads=[B_mod[v]], writes=[buf],
               allow_slow_non_contiguous=True)

    off_xh = ar.off
    xh_all = ar.alloc((3, D), BF16)
    xh = [xh_all[:, i_, :] for i_ in range(3)]
    B_xh = [Buf("xh0"), Buf("xh1"), Buf("xh2")]
    junk = ar.alloc((D,), BF16)
    B_junk = Buf("junk")
    stat = ar.alloc((64,), F32)
    B_stats = [Buf(f"stat{i}") for i in range(16)]
    statc = [0]
    m_persist = ar.mark()

    hT = ar.alloc((KC, S), BF16)
    B_hTa = [Buf(f"hTa{m}") for m in range(NCH)]
    B_hTb = [Buf(f"hTb{m}") for m in range(NCH)]
    B_hT = [None] * NCH
    m_p1 = ar.mark()
    NXT = 6
    xt = [ar.alloc((D,), F32) for _ in range(NXT)]
    B_xt = [Buf(f"xt{i}") for i in range(NXT)]
    d_xt = [sc.dsem(f"dxt{i}") for i in range(NXT)]
    for t_ in range(4):
        wslot.append(ar.alloc((4096,), BF16))
        B_ws.append(Buf(f"ws_tmp{t_}"))
        d_ws.append(sc.dsem(f"dws_tmp{t_}"))
    PT = [PS[4][:, :].bitcast(BF16), PS[5][:, :].bitcast(BF16)]

    def rms_rstd(src_ap, src_bufs, width, excl=(), jbuf=None):
        jb = junk if jbuf is None else jbuf
        slot = statc[0] % 16
        statc[0] += 1
        col = 4 * slot
        Bs = B_stats[slot]
        ss = stat[:, col:col + 1]
        sd = stat[:, col + 1:col + 2]
        rs = stat[:, col + 2:col + 3]
        sc.op("act", lambda e: e.activation(out=jb[:, 0:width], in_=src_ap, func=AF.Square, accum_out=ss),
              reads=src_bufs, writes=[B_junk, Bs] + list(excl))
        sc.op("act", lambda e: e.activation(out=sd, in_=ss, func=AF.Sqrt, scale=1.0 / width, bias=epsb),
              reads=[B_small], writes=[Bs])
        sc.op("dve", lambda e: e.reciprocal(out=rs, in_=sd), writes=[Bs])
        return rs, Bs

    def nt_A(src_ap, src_buf, it):
        rs, Bs = rms_rstd(src_ap, [src_buf], D)
        xhb = xh[it % 3]
        Bx = B_xh[it % 3]
        sc.op("dve", lambda e: e.tensor_scalar(out=xhb, in0=src_ap, scalar1=rs, scalar2=None, op0=ALU.mult),
              reads=[src_buf, Bs], writes=[Bx])
        return xhb, Bx

    def nt_B(xhb, Bx, dstT, dst_bufs, m_local):
        for half in range(2):
            bank = 4 + half

            def fn(e, half=half):
                ins = None
                for j in range(8):
                    kc = half * 8 + j
                    ins = e.transpose(out=PT[half][:, j * 128:(j + 1) * 128], in_=xhb[:, kc * 128:(kc + 1) * 128], identity=ident)
                return ins
            sc.op("pe", fn, reads=[Bx, B_small], writes=[PB[bank]])
            o = dstT[:, half * 8:(half + 1) * 8, m_local * 128:(m_local + 1) * 128]
            i_ = PT[half].rearrange("p (a b) -> p a b", b=128)
            if half == 0:
                sc.op("act", lambda e, o=o, i_=i_: e.activation(out=o, in_=i_, func=AF.Copy), writes=[PB[bank], dst_bufs[half]])
            else:
                sc.op("dve", lambda e, o=o, i_=i_: e.tensor_copy(out=o, in_=i_), writes=[PB[bank], dst_bufs[half]])

    def nt_gen(src_ap, src_buf, dstT, dst_bufs, m_local, it, pre=None):
        if pre is not None:
            pre()
        xhb, Bx = nt_A(src_ap, src_buf, it)
        yield
        nt_B(xhb, Bx, dstT, dst_bufs, m_local)

    def modulate(dstT, a_t, b_t, ab_buf, bufs_lo, bufs_hi, kc_bufs, engines):
        for kc in range(KC):
            eng = engines[kc % len(engines)]
            bufs = bufs_lo if kc < 8 else bufs_hi
            t_ = dstT[:, kc, :]
            if eng == "act":
                sc.op("act", lambda e, t_=t_, kc=kc: e.activation(out=t_, in_=t_, func=AF.Identity,
                                                                 scale=a_t[:, kc:kc + 1], bias=b_t[:, kc:kc + 1]),
                      reads=[ab_buf] + list(bufs), writes=[kc_bufs[kc]])
            else:
                sc.op(eng, lambda e, t_=t_, kc=kc: e.tensor_scalar(out=t_, in0=t_, scalar1=a_t[:, kc:kc + 1], scalar2=b_t[:, kc:kc + 1],
                                                                  op0=ALU.mult, op1=ALU.add),
                      reads=[ab_buf] + list(bufs), writes=[kc_bufs[kc]])

    def xload(m):
        i = m % NXT
        sc.dma("sp", d_xt[i], xt[i], x[m * 128:(m + 1) * 128, :], writes=[B_xt[i]])

    def p1_gen(m):
        i = m % NXT
        return nt_gen(xt[i], B_xt[i], hT, [B_hTa[m], B_hTb[m]], m, m)
    XPF = 3
    for m in range(XPF):
        xload(m)
    rounds = pipeline_rounds([p1_gen(m) for m in range(NCH)], 3)
    mods01 = [mod_gen(0, 7), mod_gen(1, 7)]
    sc_pp = ar.alloc((KC,), F32)
    B_pp = Buf("pp")
    for i in range(max(NCH + 4, 16)):
        if i + XPF < NCH:
            xload(i + XPF)
        if i < 16:
            next(mods01[i // 8])
        if i == 8:
            load_pp(b1, 0, B_pp)
        next(rounds, None)
    for _ in rounds:
        pass
    load_pp(sc_pp, 1, B_pp)
    B_ab1 = Buf("ab1")
    sc.op("dve", lambda e: e.scalar_tensor_tensor(out=a1, in0=sc_pp, scalar=1.0, in1=g1n_sb, op0=ALU.add, op1=ALU.mult),
          reads=[B_pp, B_ident], writes=[B_ab1])
    B_hTk = [Buf(f"hTk{kc}") for kc in range(KC)]
    modulate(hT, a1, b1, B_ab1, B_hTa, B_hTb, B_hTk, ["dve", "dve", "act"])

    if debug_stop in (11, 12, 13):
        sc.barrier()
        sc.replay()
        return nc
    if debug_stop == 1:
        dbg = nc.dram_tensor("dbg_hT", [128, KC, S], BF16, kind="ExternalOutput").ap()
        for kc in range(KC):
            sc.dma("sp", d_misc, dbg[:, kc, :], hT[:, kc, :], reads=B_hTk)
        sc.barrier()
        sc.replay()
        return nc
    del wslot[3:], B_ws[3:], d_ws[3:]
    ar.reset(m_p1)
    sc.barrier(skip=("pool",))
    B_fence2 = Buf("fence2")
    tok_f2 = sc.op("dve", lambda e: e.memset(stat[:, 3:4], 0.0), writes=[B_fence2])
    wslot.append(xh_all[:, 0:2, :].rearrange("p a b -> p (a b)"))
    B_ws.append(Buf("ws_alias"))
    B_ws[-1].w = tok_f2
    d_ws.append(sc.dsem("dws_alias"))
    wslot.append(ar.t[0:128, off_xh + 8192:off_xh + 16384].bitcast(BF16))
    B_ws.append(Buf("ws_alias2"))
    B_ws[-1].w = tok_f2
    d_ws.append(sc.dsem("dws_alias2"))

    m_p2 = ar.mark()
    upad = ar.alloc((2, 2, S + 16), F32)
    B_up = [[Buf(f"up{g_}{f_}") for f_ in range(2)] for g_ in range(2)]
    sa = ar.alloc((S + 16,), F32)
    sb = ar.alloc((S + 16,), F32)
    B_sa, B_sb = Buf("sa"), Buf("sb")
    invc_sb = [ar.alloc((S,), F32) for _ in range(2)]
    B_invc = [Buf("invc0"), Buf("invc1")]
    d_invc = [sc.dsem("dinvc0"), sc.dsem("dinvc1")]
    pooledT = ar.alloc((2, 2, S), BF16)
    B_pooled = [[Buf(f"pooled{g_}{f_}") for f_ in range(2)] for g_ in range(2)]
    stg = [ar.alloc((S,), BF16) for _ in range(2)]
    B_stg = [Buf("stg0"), Buf("stg1")]
    d_stg = [sc.dsem("dstg0"), sc.dsem("dstg1")]
    pw = ar.alloc((4, 2, 256), BF16)
    B_pw = Buf("pw")
    d_pw = sc.dsem("dpw")
    sc.op("dve", lambda e: e.memset(upad[:, :, :, 0:8], 0.0), writes=[b_ for bb in B_up for b_ in bb])
    sc.op("dve", lambda e: e.memset(upad[:, :, :, 8 + S:16 + S], 0.0), writes=[b_ for bb in B_up for b_ in bb])
    abank = [0]
    stgc = [0]

    def inproj_group(slab, bs, fc, tt):
        bank = abank[0] % 4
        abank[0] += 1

        def fn(e):
            ins = None
            for kc in range(KC):
                ins = e.matmul(PS[bank][:, :], lhsT=slab[:, kc, fc * 128:(fc + 1) * 128], rhs=hT[:, kc, tt * 512:(tt + 1) * 512],
                               start=(kc == 0), stop=(kc == KC - 1))
            return ins
        sc.op("pe", fn, reads=[bs] + B_hTk, writes=[PB[bank]])
        return bank

    def w_in_slab(c0):
        r_ = load_w(w_in[:, c0:c0 + 256].rearrange("(kc p) n -> p kc n", p=128), 4096, v3(256))
        precast(1)
        return r_

    def pool_gen(gi):
        g2 = gi % 2
        L = S + 16
        slab, bs = w_in_slab(gi * 256)
        if gi == 1:
            for g_ in range(4):
                sc.dma("pool", d_pw, pw[:, g_], pool_w[g_].rearrange("(cc p) d -> p cc d", p=128), reads=[B_fence2], writes=[B_pw])
        sc.dma("sp", d_invc[g2], invc_sb[g2], invc[gi], writes=[B_invc[g2]])
        for fc in range(2):
            for tt in range(4):
                bank = inproj_group(slab, bs, fc, tt)
                sc.op("act", lambda e, bank=bank, fc=fc, tt=tt: e.activation(out=upad[:, g2, fc, 8 + tt * 512:8 + (tt + 1) * 512],
                                                                          in_=PS[bank][:, :], func=AF.Copy),
                      writes=[PB[bank], B_up[g2][fc]])
        yield
        for fc in range(2):
            up = upad[:, g2, fc, :]
            sc.op("dve", lambda e, up=up: e.tensor_tensor(out=sa[:, 1:L], in0=up[:, 0:L - 1], in1=up[:, 1:L], op=ALU.add),
                  reads=[B_up[g2][fc]], writes=[B_sa])
            cur, curB, oth, othB, lo = sa, B_sa, sb, B_sb, 1
            for sh in (1, 2, 4)[:gi]:
                nlo = lo + sh
                hi = L - nlo

                def f(e, cur=cur, oth=oth, nlo=nlo, hi=hi, sh=sh):
                    return e.tensor_tensor(out=oth[:, nlo:hi], in0=cur[:, nlo - sh:hi - sh], in1=cur[:, nlo + sh:hi + sh], op=ALU.add)
                sc.op("dve", f, reads=[curB], writes=[othB])
                cur, curB, oth, othB, lo = oth, othB, cur, curB, nlo
            sc.op("dve", lambda e, cur=cur, oth=oth: e.tensor_tensor(out=oth[:, 8:8 + S], in0=cur[:, 8:8 + S], in1=invc_sb[g2], op=ALU.mult),
                  reads=[curB, B_invc[g2]], writes=[othB])
            sc.op("dve", lambda e, oth=oth, up=up, fc=fc: e.tensor_tensor(out=pooledT[:, g2, fc, :], in0=oth[:, 8:8 + S], in1=up[:, 8:8 + S], op=ALU.subtract),
                  reads=[othB, B_up[g2][fc]], writes=[B_pooled[g2][fc]])
        yield
        for dc in range(2):
            si = stgc[0] % 2
            stgc[0] += 1
            for tt in range(4):
                bank = abank[0] % 4
                abank[0] += 1

                def fn(e, bank=bank, dc=dc, tt=tt):
                    ins = None
                    for cc in range(2):
                        ins = e.matmul(PS[bank][:, :], lhsT=pw[:, gi, cc, dc * 128:(dc + 1) * 128],
                                       rhs=pooledT[:, g2, cc, tt * 512:(tt + 1) * 512], start=(cc == 0), stop=(cc == 1))
                    return ins
                sc.op("pe", fn, reads=[B_pw] + B_pooled[g2], writes=[PB[bank]])
                col = gi * 2 + dc
                sc.op("act", lambda e, bank=bank, si=si, tt=tt, col=col: e.activation(out=stg[si][:, tt * 512:(tt + 1) * 512], in_=PS[bank][:, :],
                                                                                 func=AF.Copy, scale=psc_sb[:, col:col + 1]),
                      reads=[B_ident], writes=[PB[bank], B_stg[si]])
            sc.dma("sp", d_stg[si], catT[gi * 2 + dc], stg[si], reads=[B_stg[si]], writes=[B_cat[gi * 2 + dc]])

    run_pipeline([pool_gen(gi) for gi in range(4)], 3)

    if debug_stop == 2:
        sc.barrier()
        sc.replay()
        return nc
    ar.reset(m_p2)
    sc.barrier(skip=("pool",))

    junk3 = ar.alloc((DH,), BF16)
    cm = ar.alloc((514,), F32)
    dec_sb = ar.alloc((8,), F32)
    lg = ar.alloc((8,), F32)
    cdec = ar.alloc((8,), F32)
    kd = ar.alloc((8,), F32)
    DmT = ar.alloc((4, 128), F32)
    rowf = ar.alloc((4, 128), F32)
    rowb = ar.alloc((4, 128), F32)
    ttmp = ar.alloc((128,), F32)
    B_tab = Buf("tab")
    B_tt = Buf("ttmp")
    d_c = sc.dsem("dconst")
    sc.dma("sp", d_c, cm, cmat, writes=[B_tab])
    sc.dma("sp", d_c, dec_sb, decay, writes=[B_tab])
    sc.op("act", lambda e: e.activation(out=lg, in_=dec_sb, func=AF.Exp), reads=[B_tab], writes=[B_tt])
    sc.op("dve", lambda e: e.tensor_scalar(out=lg, in0=lg, scalar1=-1.0, scalar2=None, op0=ALU.mult), reads=[B_tt], writes=[B_tt])
    sc.op("act", lambda e: e.activation(out=cdec, in_=lg, func=AF.Exp, scale=128.0), reads=[B_tt], writes=[B_tab])
    Am, Bm, rF, rB = cm[:, 0:128], cm[:, 128:256], cm[:, 256:384], cm[:, 384:512]
    cF, cB = cm[:, 512:513], cm[:, 513:514]
    for h in range(NH):
        lf = lg[:, h:h + 1]
        lb = lg[:, 4 + h:5 + h]
        sc.op("dve", lambda e, lf=lf: e.tensor_scalar(out=ttmp, in0=Am, scalar1=lf, scalar2=LN16, op0=ALU.mult, op1=ALU.add),
              reads=[B_tab, B_tt], writes=[B_tt])
        sc.op("dve", lambda e, lb=lb: e.scalar_tensor_tensor(out=ttmp, in0=Bm, scalar=lb, in1=ttmp, op0=ALU.mult, op1=ALU.add),
              reads=[B_tab, B_tt], writes=[B_tt])
        sc.op("act", lambda e, h=h: e.activation(out=DmT[:, h, :], in_=ttmp, func=AF.Exp), reads=[B_tt], writes=[B_tt])
        sc.op("dve", lambda e, lf=lf: e.tensor_scalar(out=ttmp, in0=rF, scalar1=lf, scalar2=LN16, op0=ALU.mult, op1=ALU.add),
              reads=[B_tab, B_tt], writes=[B_tt])
        sc.op("act", lambda e, h=h: e.activation(out=rowf[:, h, :], in_=ttmp, func=AF.Exp), reads=[B_tt], writes=[B_tt])
        sc.op("dve", lambda e, lb=lb: e.tensor_scalar(out=ttmp, in0=rB, scalar1=lb, scalar2=LN16, op0=ALU.mult, op1=ALU.add),
              reads=[B_tab, B_tt], writes=[B_tt])
        sc.op("act", lambda e, h=h: e.activation(out=rowb[:, h, :], in_=ttmp, func=AF.Exp), reads=[B_tt], writes=[B_tt])
        sc.op("dve", lambda e, lf=lf: e.tensor_tensor(out=ttmp[:, 0:1], in0=cF, in1=lf, op=ALU.mult), reads=[B_tab, B_tt], writes=[B_tt])
        sc.op("act", lambda e, h=h: e.activation(out=kd[:, h:h + 1], in_=ttmp[:, 0:1], func=AF.Exp), reads=[B_tt], writes=[B_tt])
        sc.op("dve", lambda e, lb=lb: e.tensor_tensor(out=ttmp[:, 0:1], in0=cB, in1=lb, op=ALU.mult), reads=[B_tab, B_tt], writes=[B_tt])
        sc.op("act", lambda e, h=h: e.activation(out=kd[:, 4 + h:5 + h], in_=ttmp[:, 0:1], func=AF.Exp), reads=[B_tt], writes=[B_tt])
    B_T = B_tt

    qT = ar.alloc((2, S), BF16)
    kT = ar.alloc((2, S), BF16)
    vtok = ar.alloc((NCH, DH), BF16)
    sgT = ar.alloc((2, S), BF16)
    Sb_all = ar.alloc((NCH, 512), BF16)
    Sf_all = ar.alloc((NCH, 512), BF16)
    rstg = [ar.alloc((2, 128), BF16) for _ in range(4)]
    B_qT = [Buf(f"qT{t}") for t in range(4)]
    B_kT = [Buf(f"kT{t}") for t in range(4)]
    B_v = [Buf(f"v{n}") for n in range(NCH)]
    B_sg = [Buf(f"sg{t}") for t in range(4)]
    B_Sb = [Buf(f"Sb{n}") for n in range(NCH)]
    B_Sf = [Buf(f"Sf{n}") for n in range(NCH)]
    B_rstg = [Buf(f"rstg{n}") for n in range(4)]
    d_rstg = [sc.dsem(f"drstg{n}") for n in range(4)]
    csb = [ar.alloc((2, 256), F32) for _ in range(2)]
    B_cs = [Buf("cs0"), Buf("cs1")]
    d_cs = [sc.dsem("dcs0"), sc.dsem("dcs1")]
    rt = [ar.alloc((512,), F32) for _ in range(2)]
    B_rt = [Buf(f"rt{i}") for i in range(2)]
    S32 = {d_: [ar.alloc((512,), F32) for _ in range(2)] for d_ in "fb"}
    B_S32 = {d_: [Buf(f"S32{d_}0"), Buf(f"S32{d_}1")] for d_ in "fb"}
    NKR = 4
    kdt = [ar.alloc((DH,), BF16) for _ in range(NKR)]
    B_kdt = [Buf(f"kdt{i}") for i in range(NKR)]
    PTm = [ar.alloc((128,), BF16) for _ in range(2)]
    B_PTm = [Buf("PTm0"), Buf("PTm1")]
    qf = [ar.alloc((2, 128), BF16) for _ in range(2)]
    qb = [ar.alloc((2, 128), BF16) for _ in range(2)]
    B_qf = [Buf("qf0"), Buf("qf1")]
    B_qb = [Buf("qb0"), Buf("qb1")]
    retn = [ar.alloc((DH,), BF16) for _ in range(2)]
    B_retn = [Buf("retn0"), Buf("retn1")]
    PTb7 = PS[7][:, :].bitcast(BF16)
    PTb3 = PS[3][:, :].bitcast(BF16)
    csc = [0]
    kdc = [0]
    dlc = [0]
    sc.op("dve", lambda e: e.memset(Sb_all[:, NCH - 1, :], 0.0), writes=[B_Sb[NCH - 1]])
    sc.op("dve", lambda e: e.memset(Sf_all[:, 0, :], 0.0), writes=[B_Sf[0]])

    def rope_evac(dstT, dstB, tt, b0, b1):
        for hf in range(2):
            i = csc[0] % 2
            csc[0] += 1
            c0 = tt * 512 + hf * 256
            sc.dma("sp", d_cs[i], csb[i], cs[:, :, c0:c0 + 256], writes=[B_cs[i]])
            co, si = csb[i][:, 0, :], csb[i][:, 1, :]
            p0, p1 = PS[b0][:, hf * 256:(hf + 1) * 256], PS[b1][:, hf * 256:(hf + 1) * 256]
            sl = slice(c0, c0 + 256)
            r0, r1 = rt[0][:, 0:256], rt[1][:, 0:256]

            def ops(i=i, co=co, si=si, p0=p0, p1=p1, sl=sl, r0=r0, r1=r1):
                sc.op("dve", lambda e: e.tensor_tensor(out=r0, in0=p0, in1=co, op=ALU.mult), reads=[B_cs[i]], writes=[PB[b0], B_rt[0]])
                sc.op("dve", lambda e: e.tensor_tensor(out=r1, in0=p1, in1=si, op=ALU.mult), reads=[B_cs[i]], writes=[PB[b1], B_rt[1]])
                sc.op("dve", lambda e: e.tensor_tensor(out=dstT[:, 0, sl], in0=r0, in1=r1, op=ALU.subtract),
                      reads=[B_rt[0], B_rt[1]], writes=[dstB[tt]])
                sc.op("dve", lambda e: e.tensor_tensor(out=r0, in0=p0, in1=si, op=ALU.mult), reads=[B_cs[i]], writes=[PB[b0], B_rt[0]])
                sc.op("dve", lambda e: e.tensor_tensor(out=r1, in0=p1, in1=co, op=ALU.mult), reads=[B_cs[i]], writes=[PB[b1], B_rt[1]])
                sc.op("dve", lambda e: e.tensor_tensor(out=dstT[:, 1, sl], in0=r0, in1=r1, op=ALU.add),
                      reads=[B_rt[0], B_rt[1]], writes=[dstB[tt]])
            ops()

    def k_tok(n, col):
        j = kdc[0] % NKR
        tb, ptb = (7, PTb7) if kdc[0] % 2 == 0 else (3, PTb3)
        kdc[0] += 1

        def fn(e):
            ins = None
            for fc in range(2):
                ins = e.transpose(out=ptb[:, fc * 128:(fc + 1) * 128], in_=kT[:, fc, n * 128:(n + 1) * 128], identity=ident)
            return ins
        sc.op("pe", fn, reads=[B_kT[n // 4], B_small], writes=[PB[tb]])
        sc.op("act", lambda e: e.activation(out=kdt[j], in_=ptb[:, 0:256], func=AF.Copy, scale=kd[:, col:col + 1]),
              reads=[B_T], writes=[PB[tb], B_kdt[j]])
        return kdt[j], B_kdt[j]

    def state_delta(kt, kB, n):
        bank = 6 if dlc[0] % 2 == 0 else 2
        dlc[0] += 1

        def fn(e):
            ins = None
            for dc in range(2):
                ins = e.matmul(PS[bank][:, dc * 256:(dc + 1) * 256], lhsT=kt[:, dc * 128:(dc + 1) * 128], rhs=vtok[:, n, :],
                               start=True, stop=True)
            return ins
        sc.op("pe", fn, reads=[kB, B_v[n]], writes=[PB[bank]])
        return bank

    def chain_gen(h, dirn, n, st):
        col = h if dirn == "f" else 4 + h
        kt, kB = k_tok(n, col)
        yield
        bank = state_delta(kt, kB, n)
        yield
        cur = st[dirn]
        nxt = 1 - cur
        st[dirn] = nxt
        sc.op("dve", lambda e: e.scalar_tensor_tensor(out=S32[dirn][nxt], in0=S32[dirn][cur], scalar=cdec[:, col:col + 1], in1=PS[bank][:, :],
                                                     op0=ALU.mult, op1=ALU.add),
              reads=[B_S32[dirn][cur], B_tab], writes=[PB[bank], B_S32[dirn][nxt]])
        if dirn == "f":
            sc.op("act", lambda e: e.activation(out=Sf_all[:, n + 1, :], in_=S32[dirn][nxt], func=AF.Copy),
                  reads=[B_S32[dirn][nxt]], writes=[B_Sf[n + 1]])
        else:
            sc.op("act", lambda e: e.activation(out=Sb_all[:, n - 1, :], in_=S32[dirn][nxt], func=AF.Copy),
                  reads=[B_S32[dirn][nxt]], writes=[B_Sb[n - 1]])

    def out_gen(h, n):
        p = n % 2
        ns = slice(n * 128, (n + 1) * 128)
        tq = n // 4
        sbank = (4, 0)[p]
        obank = (5, 1)[p]
        tbank, ptb = ((7, PTb7), (3, PTb3))[p]

        def fsc(e):
            ins = None
            for dc in range(2):
                ins = e.matmul(PS[sbank][:, 0:128], lhsT=kT[:, dc, ns], rhs=qT[:, dc, ns], start=(dc == 0), stop=(dc == 1))
            return ins
        sc.op("pe", fsc, reads=[B_kT[tq], B_qT[tq]], writes=[PB[sbank]])
        sc.op("dve", lambda e: e.tensor_tensor(out=PTm[p], in0=PS[sbank][:, 0:128], in1=DmT[:, h, :], op=ALU.mult),
              reads=[B_T], writes=[PB[sbank], B_PTm[p]])
        sc.op("dve", lambda e: e.tensor_tensor(out=qf[p], in0=qT[:, :, ns], in1=rowf[:, h:h + 1, :].to_broadcast([128, 2, 128]), op=ALU.mult),
              reads=[B_qT[tq], B_T], writes=[B_qf[p]])
        sc.op("dve", lambda e: e.tensor_tensor(out=qb[p], in0=qT[:, :, ns], in1=rowb[:, h:h + 1, :].to_broadcast([128, 2, 128]), op=ALU.mult),
              reads=[B_qT[tq], B_T], writes=[B_qb[p]])
        yield

        def fo(e):
            e.matmul(PS[obank][:, 0:256], lhsT=PTm[p], rhs=vtok[:, n, :], start=True, stop=False)
            for dc in range(2):
                e.matmul(PS[obank][:, 0:256], lhsT=qf[p][:, dc, :], rhs=Sf_all[:, n, dc * 256:(dc + 1) * 256], start=False, stop=False)
            ins = None
            for dc in range(2):
                ins = e.matmul(PS[obank][:, 0:256], lhsT=qb[p][:, dc, :], rhs=Sb_all[:, n, dc * 256:(dc + 1) * 256], start=False, stop=(dc == 1))
            return ins
        sc.op("pe", fo, reads=[B_PTm[p], B_v[n], B_qf[p], B_qb[p], B_Sf[n], B_Sb[n]], writes=[PB[obank]])
        yield
        rs, Bs = rms_rstd(PS[obank][:, 0:256], [], DH, excl=[PB[obank]], jbuf=junk3)
        sc.op("act", lambda e: e.activation(out=retn[p], in_=PS[obank][:, 0:256], func=AF.Copy, scale=rs),
              reads=[Bs], writes=[PB[obank], B_retn[p]])
        yield

        def ftr(e):
            ins = None
            for ec in range(2):
                ins = e.transpose(out=ptb[:, 512 + ec * 128:512 + (ec + 1) * 128], in_=retn[p][:, ec * 128:(ec + 1) * 128], identity=ident)
            return ins
        sc.op("pe", ftr, reads=[B_retn[p], B_small], writes=[PB[tbank]])
        r4 = n % 4
        sc.op("dve", lambda e: e.tensor_tensor(out=rstg[r4], in0=ptb[:, 512:768].rearrange("p (a b) -> p a b", b=128),
                                              in1=sgT[:, :, ns], op=ALU.mult),
              reads=[B_sg[tq]], writes=[PB[tbank], B_rstg[r4]])
        sc.dma("sp", d_rstg[r4], catT[8 + 2 * h:10 + 2 * h, :, ns].rearrange("c p t -> p c t"), rstg[r4],
               reads=[B_rstg[r4]], writes=[B_cat[8 + 2 * h], B_cat[9 + 2 * h]])

    B_pp2 = Buf("pp2")
    B_ab2 = Buf("ab2")
    for h in range(NH):
        if h == 2:
            load_pp(b2, 3, B_pp2)
        if h == 3:
            load_pp(sc2_pp, 4, B_pp2)
            sc.op("dve", lambda e: e.scalar_tensor_tensor(out=a2, in0=sc2_pp, scalar=1.0, in1=g2n_sb, op0=ALU.add, op1=ALU.mult),
                  reads=[B_pp2, B_ident], writes=[B_ab2])
        mg = mod_gen(2 + h, 6)
        for which, dstT, dstB in ((0, qT, B_qT), (1, kT, B_kT)):
            slab, bs = w_in_slab(1024 + which * 1024 + h * 256)
            for tt in range(4):
                b0 = inproj_group(slab, bs, 0, tt)
                b1_ = inproj_group(slab, bs, 1, tt)
                rope_evac(dstT, dstB, tt, b0, b1_)
            next(mg)
            next(mg)
        slab, bs = w_in_slab(3072 + h * 256)
        for n2 in range(NCH // 2):
            bank = abank[0] % 4
            abank[0] += 1

            def fn(e, slab=slab, n2=n2, bank=bank):
                ins = None
                for j in range(2):
                    n = n2 * 2 + j
                    for kc in range(KC):
                        ins = e.matmul(PS[bank][:, j * 256:(j + 1) * 256], lhsT=hT[:, kc, n * 128:(n + 1) * 128], rhs=slab[:, kc, :],
                                       start=(kc == 0), stop=(kc == KC - 1))
                return ins
            sc.op("pe", fn, reads=[bs] + B_hTk, writes=[PB[bank]])
            sc.op("act", lambda e, bank=bank, n2=n2: e.activation(out=vtok[:, n2 * 2:n2 * 2 + 2, :],
                                                                 in_=PS[bank][:, :].rearrange("p (a b) -> p a b", b=256), func=AF.Copy),
                  writes=[PB[bank], B_v[n2 * 2], B_v[n2 * 2 + 1]])
        next(mg)
        next(mg)
        slab, bs = w_in_slab(4096 + h * 256)
        for fc in range(2):
            for tt in range(4):
                bank = inproj_group(slab, bs, fc, tt)
                sc.op("act", lambda e, bank=bank, fc=fc, tt=tt: e.activation(out=sgT[:, fc, tt * 512:(tt + 1) * 512], in_=PS[bank][:, :], func=AF.Silu),
                      writes=[PB[bank], B_sg[tt]])

        next(mg)
        next(mg)
        st = {"f": 0, "b": 0}
        sc.op("dve", lambda e: e.memset(S32["f"][0], 0.0), writes=[B_S32["f"][0]])
        sc.op("dve", lambda e: e.memset(S32["b"][0], 0.0), writes=[B_S32["b"][0]])
        gens = []
        for i in range(NCH - 1):
            gens.append(chain_gen(h, "f", i, st))
            gens.append(chain_gen(h, "b", NCH - 1 - i, st))
        run_pipeline(gens, 6)
        run_pipeline([out_gen(h, n) for n in range(NCH)], 4)

    peak3 = ar.off
    if debug_stop == 3:
        sc.barrier()
        sc.replay()
        return nc
    precast(len(pre_jobs))
    for b_ in B_wsc:
        b_.w = (d_pre.h, d_pre.val)
    ar.reset(m_persist)
    sc.barrier(skip=("pool",))
    B_fence4 = Buf("fence4")
    tok_f4 = sc.op("dve", lambda e: e.memset(stat[:, 7:8], 0.0), writes=[B_fence4])

    g1_bc = ar.alloc((D,), F32)
    g2_bc = ar.alloc((D,), F32)
    fg_bc = ar.alloc((D,), F32)
    B_bc = Buf("bc")
    B_bc2 = Buf("bc2")
    d_bc = sc.dsem("dbc")
    d_bc2 = sc.dsem("dbc2")
    x1 = ar.alloc((4, D), F32)
    B_x1 = [Buf(f"x1_{m}") for m in range(4)]
    d_x1 = [sc.dsem(f"dx1_{m}") for m in range(4)]
    d_o = [sc.dsem(f"dout{m}") for m in range(4)]
    ctile = ar.alloc((KC, 512), BF16)
    B_c = Buf("ctile")
    B_h2 = [[Buf(f"h2_{m}_{hf}") for hf in range(2)] for m in range(4)]
    B_h2k = [Buf(f"h2k{kc}") for kc in range(KC)]
    B_h2all = [b_ for bb in B_h2 for b_ in bb] + B_h2k
    d_ct = sc.dsem("dct")
    actT = ar.alloc((NFC, 512), BF16)
    B_act = [Buf(f"act{j}") for j in range(NFC)]
    sgt = [ar.alloc((512,), F32) for _ in range(2)]
    B_sgt = [Buf("sgt0"), Buf("sgt1")]
    tmpA = [ar.alloc((512,), F32) for _ in range(2)]
    B_tmpA = [Buf("tmpA0"), Buf("tmpA1")]
    while ar.off + 8192 + 64 <= ar.n:
        wslot.append(ar.alloc((4096,), BF16))
        B_ws.append(Buf(f"ws{len(B_ws)}"))
        B_ws[-1].w = tok_f4
        d_ws.append(sc.dsem(f"dws{len(d_ws)}"))
    ring4 = [0, 1, 2] + list(range(5, len(wslot)))

    def load_w4(src_ap, ncols_total, view, sid=None):
        i = ring4[slot_ctr[0] % len(ring4)]
        slot_ctr[0] += 1
        flat = wslot[i][:, 0:ncols_total]
        dst = view(flat)
        if sid is None:
            sc.dma("pool", d_ws[i], dst, src_ap, writes=[B_ws[i]])
        else:
            sc.dma("pool", d_ws[i], flat, wsc[sid][:, 0:ncols_total], reads=[B_wsc[sid]], writes=[B_ws[i]])
        return dst, B_ws[i]

    tctr = [0]
    def load_ctile(tt):
        sc.dma("sp", d_ct, ctile, catT[:, :, tt * 512:(tt + 1) * 512].rearrange("kc p t -> p kc t"), reads=B_cat, writes=[B_c] + B_h2all)
    load_ctile(0)
    for m in range(4):
        sc.dma("sp", d_x1[m], x1[:, m, :], x[m * 128:(m + 1) * 128, :], writes=[B_x1[m]])
    sc.dma("sp", d_bc, g1_bc, modrows[2].partition_broadcast(128), reads=[B_mod[2]], writes=[B_bc])
    sc.dma("sp", d_bc2, g2_bc, modrows[5].partition_broadcast(128), reads=[B_mod[5]], writes=[B_bc2])
    sc.dma("sp", d_bc2, fg_bc, fgr.partition_broadcast(128), writes=[B_bc2])
    for tt in range(4):
        for m in range(4):
            r0 = (tt * 4 + m) * 128
            if tt > 0:
                sc.dma("sp", d_x1[m], x1[:, m, :], x[r0:r0 + 128, :], writes=[B_x1[m]])
        ob = 0
        for db in range(4):
            ds_ = slice(db * 512, (db + 1) * 512)
            halves = []
            for hf in range(2):
                halves.append(load_w4(None, 4096, v3(512), sid=db * 2 + hf))
            for m in range(4):
                bank = ob % 8
                ob += 1

                def fn(e, halves=halves, m=m, bank=bank):
                    ins = None
                    for kc in range(KC):
                        slab = halves[kc // 8][0]
                        ins = e.matmul(PS[bank][:, :], lhsT=ctile[:, kc, m * 128:(m + 1) * 128], rhs=slab[:, kc % 8, :],
                                       start=(kc == 0), stop=(kc == KC - 1))
                    return ins
                sc.op("pe", fn, reads=[halves[0][1], halves[1][1], B_c], writes=[PB[bank]])
                ti = tctr[0] % 2
                tctr[0] += 1
                sc.op("dve", lambda e, bank=bank, ti=ti, ds_=ds_: e.tensor_tensor(out=tmpA[ti], in0=PS[bank][:, :], in1=g1_bc[:, ds_], op=ALU.mult),
                      reads=[B_bc], writes=[PB[bank], B_tmpA[ti]])
                sc.op("dve", lambda e, m=m, ti=ti, ds_=ds_: e.tensor_tensor(out=x1[:, m, ds_], in0=x1[:, m, ds_], in1=tmpA[ti], op=ALU.add),
                      reads=[B_tmpA[ti]], writes=[B_x1[m]])
        sc.op("dve", lambda e: e.memset(stat[:, 3:4], 0.0), writes=[B_c] + B_h2all)
        run_pipeline([nt_gen(x1[:, m, :], B_x1[m], ctile, B_h2[m], m, m) for m in range(4)], 2)
        modulate(ctile, a2, b2, B_ab2, [B_h2[m][0] for m in range(4)], [B_h2[m][1] for m in range(4)], B_h2k, ["dve", "act"])
        for jp in range(NFC // 2):
            cs_ = slice(jp * 256, (jp + 1) * 256)
            gsl, gB = load_w4(w_gate[:, cs_].rearrange("(kc p) n -> p kc n", p=128), 4096, v3(256))
            usl, uB = load_w4(w_up[:, cs_].rearrange("(kc p) n -> p kc n", p=128), 4096, v3(256))
            for jj in range(2):
                j = jp * 2 + jj
                gb, ub = 4 + j % 2, 6 + j % 2
                for slab, sB, bank in ((gsl, gB, gb), (usl, uB, ub)):
                    def fn(e, slab=slab, bank=bank, jj=jj):
                        ins = None
                        for kc in range(KC):
                            ins = e.matmul(PS[bank][:, :], lhsT=slab[:, kc, jj * 128:(jj + 1) * 128], rhs=ctile[:, kc, :],
                                           start=(kc == 0), stop=(kc == KC - 1))
                        return ins
                    sc.op("pe", fn, reads=[sB] + B_h2k, writes=[PB[bank]])
                si = j % 2
                sc.op("act", lambda e, si=si, gb=gb: e.activation(out=sgt[si], in_=PS[gb][:, :], func=AF.Silu),
                      writes=[PB[gb], B_sgt[si]])
                sc.op("dve", lambda e, si=si, ub=ub, j=j: e.tensor_tensor(out=actT[:, j, :], in0=PS[ub][:, :], in1=sgt[si], op=ALU.mult),
                      reads=[B_sgt[si]], writes=[PB[ub], B_act[j]])
        if tt < 3:
            load_ctile(tt + 1)
        for db in range(4):
            ds_ = slice(db * 512, (db + 1) * 512)
            bo = 4 * (db % 2)
            for jg in range(NFC // 4):
                slab, sB = load_w4(None, 2048, v3(512), sid=30 + db * 11 + jg)

                def fn(e, slab=slab, jg=jg, bo=bo):
                    ins = None
                    for a in range(4):
                        j = jg * 4 + a
                        for m in range(4):
                            ins = e.matmul(PS[bo + m][:, :], lhsT=actT[:, j, m * 128:(m + 1) * 128], rhs=slab[:, a, :],
                                           start=(j == 0), stop=(j == NFC - 1))
                    return ins
                sc.op("pe", fn, reads=[sB] + B_act[jg * 4:(jg + 1) * 4], writes=PB[bo:bo + 4])
            for m in range(4):
                ti = tctr[0] % 2
                tctr[0] += 1
                sc.op("dve", lambda e, m=m, ti=ti, ds_=ds_, bo=bo: e.tensor_tensor(out=tmpA[ti], in0=PS[bo + m][:, :], in1=g2_bc[:, ds_], op=ALU.mult),
                      reads=[B_bc2], writes=[PB[bo + m], B_tmpA[ti]])
                sc.op("dve", lambda e, m=m, ti=ti, ds_=ds_: e.tensor_tensor(out=x1[:, m, ds_], in0=x1[:, m, ds_], in1=tmpA[ti], op=ALU.add),
                      reads=[B_tmpA[ti]], writes=[B_x1[m]])
        for m in range(4):
            rs, Bs = rms_rstd(x1[:, m, :], [B_x1[m]], D)
            sc.op("dve", lambda e, m=m, rs=rs: e.scalar_tensor_tensor(out=x1[:, m, :], in0=x1[:, m, :], scalar=rs, in1=fg_bc, op0=ALU.mult, op1=ALU.mult),
                  reads=[Bs, B_bc2], writes=[B_x1[m]])
            r0 = (tt * 4 + m) * 128
            sc.dma("sp", d_o[m], out[r0:r0 + 128, :], x1[:, m, :], reads=[B_x1[m]])

    sc.barrier()
    sc.replay()
    print("arena peaks", peak3, ar.off, "ops", sc.cnt, "sems", len(sc.dsems) + 5)
    return nc


def _consts():
    half = 128
    inv = (1.0 / (np.float32(10000.0) ** np.linspace(0.0, 1.0, half, dtype=np.float32))).astype(np.float32)
    ang = (np.arange(S, dtype=np.float32)[:, None] * inv[None, :]).astype(np.float32)
    cs = np.stack([np.cos(ang).T, np.sin(ang).T], axis=1).astype(np.float32)
    j = np.arange(128)[:, None]
    i = np.arange(128)[None, :]
    cm = np.zeros((128, 514), np.float32)
    cm[:, 0:128] = np.maximum(i - j, 0)
    cm[:, 128:256] = np.maximum(j - i, 0)
    cm[:, 256:384] = (i + 1)
    cm[:, 384:512] = (128 - i)
    cm[:, 512] = 127 - np.arange(128)
    cm[:, 513] = np.arange(128)
    t = np.arange(S)
    invc = np.zeros((4, 128, S), np.float32)
    for gi, w in enumerate((2, 4, 8, 16)):
        lo = np.clip(t - w // 2, 0, S)
        hi = np.clip(t + w // 2, 0, S)
        invc[gi] = (1.0 / (hi - lo).astype(np.float32))[None, :]
    return cs, cm, invc, np.eye(128, dtype=np.float32)


_NC_CACHE = {}


def kernel(x, c, w_ada, b_ada, norm1_g, w_in, pool_w, pool_scale, ret_decay_fwd, ret_decay_bwd,
           w_out, norm2_g, w_gate, w_up, w_down, final_g):
    f = lambda a: np.ascontiguousarray(np.asarray(a, dtype=np.float32))
    x, c = f(x), f(c)
    cs, cm, invc, identf = _consts()
    pp = lambda v: np.ascontiguousarray(v.reshape(-1, 128).T)
    shared = {
        "w_ada": f(w_ada)[0], "b_ada": f(b_ada)[0].reshape(1, -1),
        "g1n": pp(f(norm1_g)[0]), "g2n": pp(f(norm2_g)[0]), "fg": f(final_g),
        "g1row": f(norm1_g)[0], "g2row": f(norm2_g)[0],
        "w_in": f(w_in)[0], "pool_w": f(pool_w)[0], "pscale": pp(f(pool_scale)[0]),
        "decay": np.ascontiguousarray(np.broadcast_to(
            np.concatenate([f(ret_decay_fwd)[0], f(ret_decay_bwd)[0]])[None, :], (128, 8))),
        "w_out": f(w_out)[0], "w_gate": f(w_gate)[0], "w_up": f(w_up)[0], "w_down": f(w_down)[0],
        "cs": cs, "cmat": cm, "invc": invc, "identf": identf,
    }
    if "nc" not in _NC_CACHE:
        _NC_CACHE["nc"] = build_nc()
    nc = _NC_CACHE["nc"]
    in_maps = []
    for b in range(NCORES):
        m = dict(shared)
        m["x"] = x[b]
        m["cT"] = pp(c[b])
        in_maps.append(m)
    res = run_bass_kernel_spmd(nc, in_maps, core_ids=list(range(NCORES)))
    return np.stack([np.asarray(r["out"], dtype=np.float32) for r in res.results], axis=0)
```

```python
import numpy as np
import concourse.bass as bass
import concourse.mybir as mybir
from concourse.bass_utils import run_bass_kernel_spmd

F32 = mybir.dt.float32
BF16 = mybir.dt.bfloat16
U8 = mybir.dt.uint8
ALU = mybir.AluOpType
AF = mybir.ActivationFunctionType

S = 2048
D = 2048
KC = 16
DFF = 5632
NFC = 44
INW = 5120
CH = 128
NCH = 16
NH = 4
DH = 256
EPS = 1e-6
NCORES = 8


class Buf:
    def __init__(self, name):
        self.name = name
        self.w = None
        self.r = {}


class DSem:
    def __init__(self, nc, name):
        self.h = nc.alloc_semaphore(name)
        self.val = 0


class Sched:
    ENG = ("pe", "act", "dve", "pool", "sp")

    def __init__(self, nc):
        self.nc = nc
        self.sem = {e: nc.alloc_semaphore("eng_" + e) for e in self.ENG}
        self.cnt = {e: 0 for e in self.ENG}
        self.seen = {e: {} for e in self.ENG}
        self.prog = {e: [] for e in self.ENG}
        self.dsems = []

    def dsem(self, name):
        d = DSem(self.nc, name)
        self.dsems.append(d)
        return d

    def _deps(self, e, reads, writes):
        need = {}

        def add(tok):
            if tok is None:
                return
            s, v = tok
            k = id(s)
            if k not in need or need[k][1] < v:
                need[k] = (s, v)

        for b in reads:
            add(b.w)
        for b in writes:
            add(b.w)
            for t in b.r.values():
                add(t)
        out = []
        for k, (s, v) in need.items():
            if e == "pe" and s is self.sem["pe"]:
                continue
            if self.seen[e].get(k, 0) >= v:
                continue
            self.seen[e][k] = v
            out.append((s, v))
        return out

    def _mark(self, tok, reads, writes):
        for b in reads:
            b.r[id(tok[0])] = tok
        for b in writes:
            b.w = tok
            b.r = {}

    def op(self, e, fn, reads=(), writes=()):
        waits = self._deps(e, reads, writes)
        self.cnt[e] += 1
        tok = (self.sem[e], self.cnt[e])

        def emit(eng, waits=waits, fn=fn, s=tok[0]):
            for ws, wv in waits:
                eng.wait_ge(ws, wv)
            fn(eng).then_inc(s, 1)

        self.prog[e].append(emit)
        self._mark(tok, reads, writes)
        return tok

    def dma(self, q, dsem, out_ap, in_ap, reads=(), writes=(), **kw):
        waits = self._deps(q, reads, writes)
        dsem.val += 16
        tok = (dsem.h, dsem.val)

        def emit(eng, waits=waits, s=dsem.h):
            for ws, wv in waits:
                eng.wait_ge(ws, wv)
            eng.dma_start(out=out_ap, in_=in_ap, **kw).then_inc(s, 16)

        self.prog[q].append(emit)
        self._mark(tok, reads, writes)
        return tok

    def barrier(self, skip=()):
        toks = [(self.sem[e], self.cnt[e]) for e in self.ENG if self.cnt[e] > 0]
        toks += [(d.h, d.val) for d in self.dsems if d.val > 0]
        for e in self.ENG:
            if e in skip:
                continue
            waits = []
            for s, v in toks:
                k = id(s)
                if self.seen[e].get(k, 0) >= v:
                    continue
                if e == "pe" and s is self.sem["pe"]:
                    continue
                self.seen[e][k] = v
                waits.append((s, v))

            def emit(eng, waits=waits):
                for ws, wv in waits:
                    eng.wait_ge(ws, wv)

            self.prog[e].append(emit)

    def replay(self):
        nc = self.nc
        with nc.Block() as blk:
            @blk.tensor
            def _(e):
                for f in self.prog["pe"]:
                    f(e)

            @blk.scalar
            def _(e):
                for f in self.prog["act"]:
                    f(e)

            @blk.vector
            def _(e):
                for f in self.prog["dve"]:
                    f(e)

            @blk.gpsimd
            def _(e):
                for f in self.prog["pool"]:
                    f(e)

            @blk.sync
            def _(e):
                for f in self.prog["sp"]:
                    f(e)


class Arena:
    def __init__(self, nc, nbytes):
        self.t = nc.alloc_sbuf_tensor("arena", [128, nbytes], U8)
        self.n = nbytes
        self.off = 0

    def mark(self):
        return self.off

    def reset(self, m):
        self.off = m

    def alloc(self, shape, dtype, parts=128):
        esz = 2 if dtype == BF16 else 4
        n = int(np.prod(shape)) * esz
        n = (n + 63) // 64 * 64
        assert self.off + n <= self.n, f"SBUF arena overflow {self.off}+{n}>{self.n}"
        ap = self.t[0:parts, self.off:self.off + n]
        self.off += n
        ap = ap.bitcast(dtype)
        ne = int(np.prod(shape))
        ap = ap[:, 0:ne]
        if len(shape) == 2:
            ap = ap.rearrange("p (a b) -> p a b", b=shape[1])
        elif len(shape) == 3:
            ap = ap.rearrange("p (a b c) -> p a b c", b=shape[1], c=shape[2])
        return ap


def run_pipeline(gens, window):
    for _ in pipeline_rounds(gens, window):
        pass


def pipeline_rounds(gens, window):
    active = []
    it = iter(gens)
    pending = True
    while pending or active:
        if pending and len(active) < window:
            try:
                active.append(next(it))
            except StopIteration:
                pending = False
        for g in reversed(list(active)):
            try:
                next(g)
            except StopIteration:
                active.remove(g)
        yield


LN16 = -2.772588722239781


def build_nc(debug_stop=None):
    nc = bass.Bass("TRN2", target_bir_lowering=False)
    dbg_kind = "ExternalOutput" if debug_stop else "Internal"

    def din(name, shape, dt=F32):
        return nc.dram_tensor(name, shape, dt, kind="ExternalInput").ap()

    x = din("x", [S, D])
    cT = din("cT", [128, KC])
    w_ada = din("w_ada", [D, 6 * D])
    b_ada = din("b_ada", [1, 6 * D])
    g1n = din("g1n", [128, KC])
    g2n = din("g2n", [128, KC])
    fgr = din("fg", [D])
    g1row = din("g1row", [D])
    g2row = din("g2row", [D])
    w_in = din("w_in", [D, INW])
    pool_w = din("pool_w", [4, 256, 256])
    pscale = din("pscale", [128, 8])
    decay = din("decay", [128, 8])
    w_out = din("w_out", [D, D])
    w_gate = din("w_gate", [D, DFF])
    w_up = din("w_up", [D, DFF])
    w_down = din("w_down", [DFF, D])
    cs = din("cs", [128, 2, S])
    cmat = din("cmat", [128, 514])
    invc = din("invc", [4, 128, S])
    identf = din("identf", [128, 128])
    out = nc.dram_tensor("out", [S, D], F32, kind="ExternalOutput").ap()
    catT = nc.dram_tensor("catT", [16, 128, S], BF16, kind=dbg_kind).ap()
    modrows = nc.dram_tensor("modrows", [6, D], F32, kind=dbg_kind).ap()

    sc = Sched(nc)
    ar = Arena(nc, 207 * 1024)
    PS = [nc.alloc_psum_tensor(f"ps{i}", [128, 512], F32) for i in range(8)]
    PB = [Buf(f"psb{i}") for i in range(8)]

    NSC = 8 + 22 + 44
    wsc = nc.dram_tensor("wsc", [NSC, 128, 4096], BF16).ap()
    B_wsc = [Buf(f"wsc{i}") for i in range(NSC)]
    d_pre = sc.dsem("dprecast")
    pre_jobs = []
    for db in range(4):
        for hf in range(2):
            pre_jobs.append((db * 2 + hf, wsc[db * 2 + hf][:, 0:4096].rearrange("p (kc n) -> p kc n", n=512),
                             w_out[hf * 1024:(hf + 1) * 1024, db * 512:(db + 1) * 512].rearrange("(kc p) n -> p kc n", p=128)))
    for db in range(4):
        for jg in range(11):
            sid = 30 + db * 11 + jg
            pre_jobs.append((sid, wsc[sid][:, 0:2048].rearrange("p (a n) -> p a n", n=512),
                             w_down[jg * 512:(jg + 1) * 512, db * 512:(db + 1) * 512].rearrange("(a p) n -> p a n", p=128)))
    pre_it = iter(pre_jobs)

    def precast(n):
        for _ in range(n):
            job = next(pre_it, None)
            if job is None:
                return
            sid, dst, src = job
            sc.dma("pool", d_pre, dst, src, writes=[B_wsc[sid]])

    ident = ar.alloc((128,), BF16)
    B_ident = Buf("ident")
    epsb = ar.alloc((1,), F32)
    a1 = ar.alloc((KC,), F32)
    b1 = ar.alloc((KC,), F32)
    a2 = ar.alloc((KC,), F32)
    b2 = ar.alloc((KC,), F32)
    g1n_sb = ar.alloc((KC,), F32)
    g2n_sb = ar.alloc((KC,), F32)
    psc_sb = ar.alloc((8,), F32)
    cact = ar.alloc((KC,), BF16)
    sc2_pp = ar.alloc((KC,), F32)
    B_small = Buf("small")
    B_a1, B_a2 = Buf("a1"), Buf("a2")
    B_cact = Buf("cact")
    B_mod = [Buf(f"modrow{v}") for v in range(6)]
    B_cat = [Buf(f"cat{i}") for i in range(16)]
    d_mod = [sc.dsem(f"dmod{v}") for v in range(6)]
    d_misc = sc.dsem("dmisc")
    d_misc2 = sc.dsem("dmisc2")

    NSLOT = 3
    wslot = [ar.alloc((4096,), BF16) for _ in range(NSLOT)]
    B_ws = [Buf(f"ws{i}") for i in range(NSLOT)]
    d_ws = [sc.dsem(f"dws{i}") for i in range(NSLOT)]
    slot_ctr = [0]

    def load_w(src_ap, ncols_total, view):
        i = slot_ctr[0] % len(wslot)
        slot_ctr[0] += 1
        dst = wslot[i][:, 0:ncols_total]
        dst = view(dst)
        sc.dma("pool", d_ws[i], dst, src_ap, writes=[B_ws[i]])
        return dst, B_ws[i]

    def v3(b):
        return lambda ap: ap.rearrange("p (a b) -> p a b", b=b)

    tmpf = ar.alloc((128,), F32)
    cT_sb = ar.alloc((KC,), F32)
    sc.dma("sp", d_misc, tmpf, identf, writes=[B_ident])
    sc.dma("sp", d_misc, cT_sb, cT, writes=[B_ident])
    sc.dma("sp", d_misc, g1n_sb, g1n, writes=[B_ident])
    sc.dma("sp", d_misc, g2n_sb, g2n, writes=[B_ident])
    sc.dma("sp", d_misc, psc_sb, pscale, writes=[B_ident])
    sc.op("dve", lambda e: e.tensor_copy(out=ident, in_=tmpf), reads=[B_ident], writes=[B_small])
    sc.op("dve", lambda e: e.memset(epsb, EPS), writes=[B_small])
    sc.op("act", lambda e: e.activation(out=cact, in_=cT_sb, func=AF.Silu), reads=[B_ident], writes=[B_cact])

    brow = [ar.alloc((256,), F32, parts=1) for _ in range(2)]
    B_brow = [Buf("brow0"), Buf("brow1")]
    d_brow = [sc.dsem("dbrow0"), sc.dsem("dbrow1")]
    mrow = [ar.alloc((256,), F32, parts=1) for _ in range(2)]
    B_mrow = [Buf("mrow0"), Buf("mrow1")]
    mctr = [0]

    def mod_gen(v, bank):
        for cb in range(8):
            mod_slab(v, (bank, 13 - bank)[cb % 2], cb)
            yield

    def mod_vector(v, bank):
        for cb in range(8):
            mod_slab(v, bank, cb)

    def mod_slab(v, bank, cb):
        if True:
            c0 = v * D + cb * 256
            slab, bs = load_w(w_ada[:, c0:c0 + 256].rearrange("(kc p) n -> p kc n", p=128), 4096, v3(256))
            if v >= 2:
                precast(1)
            j = mctr[0] % 2
            mctr[0] += 1
            sc.dma("sp", d_brow[j], brow[j], b_ada[0:1, c0:c0 + 256], writes=[B_brow[j]])

            def fn(e, slab=slab):
                ins = None
                for kc in range(KC):
                    ins = e.matmul(PS[bank][0:1, 0:256], lhsT=cact[:, kc:kc + 1], rhs=slab[:, kc, :],
                                   start=(kc == 0), stop=(kc == KC - 1))
                return ins
            sc.op("pe", fn, reads=[bs, B_cact], writes=[PB[bank]])
            sc.op("dve", lambda e, j=j: e.tensor_tensor(out=mrow[j], in0=PS[bank][0:1, 0:256], in1=brow[j], op=ALU.add),
                  reads=[B_brow[j]], writes=[PB[bank], B_mrow[j]])
            sc.dma("sp", d_mod[v], modrows[v:v + 1, cb * 256:(cb + 1) * 256], mrow[j], reads=[B_mrow[j]], writes=[B_mod[v]])

    def load_pp(dst, v, buf):
        sc.dma("sp", d_misc2, dst, modrows[v].rearrange("(kc p) -> p kc", p=128), reads=[B_mod[v]], writes=[buf],
               allow_slow_non_contiguous=True)

    off_xh = ar.off
    xh_all = ar.alloc((3, D), BF16)
    xh = [xh_all[:, i_, :] for i_ in range(3)]
    B_xh = [Buf("xh0"), Buf("xh1"), Buf("xh2")]
    junk = ar.alloc((D,), BF16)
    B_junk = Buf("junk")
    stat = ar.alloc((64,), F32)
    B_stats = [Buf(f"stat{i}") for i in range(16)]
    statc = [0]
    m_persist = ar.mark()

    hT = ar.alloc((KC, S), BF16)
    B_hTa = [Buf(f"hTa{m}") for m in range(NCH)]
    B_hTb = [Buf(f"hTb{m}") for m in range(NCH)]
    B_hT = [None] * NCH
    m_p1 = ar.mark()
    NXT = 6
    xt = [ar.alloc((D,), F32) for _ in range(NXT)]
    B_xt = [Buf(f"xt{i}") for i in range(NXT)]
    d_xt = [sc.dsem(f"dxt{i}") for i in range(NXT)]
    for t_ in range(4):
        wslot.append(ar.alloc((4096,), BF16))
        B_ws.append(Buf(f"ws_tmp{t_}"))
        d_ws.append(sc.dsem(f"dws_tmp{t_}"))
    PT = [PS[4][:, :].bitcast(BF16), PS[5][:, :].bitcast(BF16)]

    def rms_rstd(src_ap, src_bufs, width, excl=(), jbuf=None):
        jb = junk if jbuf is None else jbuf
        slot = statc[0] % 16
        statc[0] += 1
        col = 4 * slot
        Bs = B_stats[slot]
        ss = stat[:, col:col + 1]
        sd = stat[:, col + 1:col + 2]
        rs = stat[:, col + 2:col + 3]
        sc.op("act", lambda e: e.activation(out=jb[:, 0:width], in_=src_ap, func=AF.Square, accum_out=ss),
              reads=src_bufs, writes=[B_junk, Bs] + list(excl))
        sc.op("act", lambda e: e.activation(out=sd, in_=ss, func=AF.Sqrt, scale=1.0 / width, bias=epsb),
              reads=[B_small], writes=[Bs])
        sc.op("dve", lambda e: e.reciprocal(out=rs, in_=sd), writes=[Bs])
        return rs, Bs

    def nt_A(src_ap, src_buf, it):
        rs, Bs = rms_rstd(src_ap, [src_buf], D)
        xhb = xh[it % 3]
        Bx = B_xh[it % 3]
        sc.op("dve", lambda e: e.tensor_scalar(out=xhb, in0=src_ap, scalar1=rs, scalar2=None, op0=ALU.mult),
              reads=[src_buf, Bs], writes=[Bx])
        return xhb, Bx

    def nt_B(xhb, Bx, dstT, dst_bufs, m_local):
        for half in range(2):
            bank = 4 + half

            def fn(e, half=half):
                ins = None
                for j in range(8):
                    kc = half * 8 + j
                    ins = e.transpose(out=PT[half][:, j * 128:(j + 1) * 128], in_=xhb[:, kc * 128:(kc + 1) * 128], identity=ident)
                return ins
            sc.op("pe", fn, reads=[Bx, B_small], writes=[PB[bank]])
            o = dstT[:, half * 8:(half + 1) * 8, m_local * 128:(m_local + 1) * 128]
            i_ = PT[half].rearrange("p (a b) -> p a b", b=128)
            if half == 0:
                sc.op("act", lambda e, o=o, i_=i_: e.activation(out=o, in_=i_, func=AF.Copy), writes=[PB[bank], dst_bufs[half]])
            else:
                sc.op("dve", lambda e, o=o, i_=i_: e.tensor_copy(out=o, in_=i_), writes=[PB[bank], dst_bufs[half]])

    def nt_gen(src_ap, src_buf, dstT, dst_bufs, m_local, it, pre=None):
        if pre is not None:
            pre()
        xhb, Bx = nt_A(src_ap, src_buf, it)
        yield
        nt_B(xhb, Bx, dstT, dst_bufs, m_local)

    def modulate(dstT, a_t, b_t, ab_buf, bufs_lo, bufs_hi, kc_bufs, engines):
        for kc in range(KC):
            eng = engines[kc % len(engines)]
            bufs = bufs_lo if kc < 8 else bufs_hi
            t_ = dstT[:, kc, :]
            if eng == "act":
                sc.op("act", lambda e, t_=t_, kc=kc: e.activation(out=t_, in_=t_, func=AF.Identity,
                                                                 scale=a_t[:, kc:kc + 1], bias=b_t[:, kc:kc + 1]),
                      reads=[ab_buf] + list(bufs), writes=[kc_bufs[kc]])
            else:
                sc.op(eng, lambda e, t_=t_, kc=kc: e.tensor_scalar(out=t_, in0=t_, scalar1=a_t[:, kc:kc + 1], scalar2=b_t[:, kc:kc + 1],
                                                                  op0=ALU.mult, op1=ALU.add),
                      reads=[ab_buf] + list(bufs), writes=[kc_bufs[kc]])

    def xload(m):
        i = m % NXT
        sc.dma("sp", d_xt[i], xt[i], x[m * 128:(m + 1) * 128, :], writes=[B_xt[i]])

    def p1_gen(m):
        i = m % NXT
        return nt_gen(xt[i], B_xt[i], hT, [B_hTa[m], B_hTb[m]], m, m)
    XPF = 3
    for m in range(XPF):
        xload(m)
    rounds = pipeline_rounds([p1_gen(m) for m in range(NCH)], 3)
    mods01 = [mod_gen(0, 7), mod_gen(1, 7)]
    sc_pp = ar.alloc((KC,), F32)
    B_pp = Buf("pp")
    for i in range(max(NCH + 4, 16)):
        if i + XPF < NCH:
            xload(i + XPF)
        if i < 16:
            next(mods01[i // 8])
        if i == 8:
            load_pp(b1, 0, B_pp)
        next(rounds, None)
    for _ in rounds:
        pass
    load_pp(sc_pp, 1, B_pp)
    B_ab1 = Buf("ab1")
    sc.op("dve", lambda e: e.scalar_tensor_tensor(out=a1, in0=sc_pp, scalar=1.0, in1=g1n_sb, op0=ALU.add, op1=ALU.mult),
          reads=[B_pp, B_ident], writes=[B_ab1])
    B_hTk = [Buf(f"hTk{kc}") for kc in range(KC)]
    modulate(hT, a1, b1, B_ab1, B_hTa, B_hTb, B_hTk, ["dve", "dve", "act"])

    if debug_stop in (11, 12, 13):
        sc.barrier()
        sc.replay()
        return nc
    if debug_stop == 1:
        dbg = nc.dram_tensor("dbg_hT", [128, KC, S], BF16, kind="ExternalOutput").ap()
        for kc in range(KC):
            sc.dma("sp", d_misc, dbg[:, kc, :], hT[:, kc, :], reads=B_hTk)
        sc.barrier()
        sc.replay()
        return nc
    del wslot[3:], B_ws[3:], d_ws[3:]
    ar.reset(m_p1)
    sc.barrier(skip=("pool",))
    B_fence2 = Buf("fence2")
    tok_f2 = sc.op("dve", lambda e: e.memset(stat[:, 3:4], 0.0), writes=[B_fence2])
    wslot.append(xh_all[:, 0:2, :].rearrange("p a b -> p (a b)"))
    B_ws.append(Buf("ws_alias"))
    B_ws[-1].w = tok_f2
    d_ws.append(sc.dsem("dws_alias"))
    wslot.append(ar.t[0:128, off_xh + 8192:off_xh + 16384].bitcast(BF16))
    B_ws.append(Buf("ws_alias2"))
    B_ws[-1].w = tok_f2
    d_ws.append(sc.dsem("dws_alias2"))

    m_p2 = ar.mark()
    upad = ar.alloc((2, 2, S + 16), F32)
    B_up = [[Buf(f"up{g_}{f_}") for f_ in range(2)] for g_ in range(2)]
    sa = ar.alloc((S + 16,), F32)
    sb = ar.alloc((S + 16,), F32)
    B_sa, B_sb = Buf("sa"), Buf("sb")
    invc_sb = [ar.alloc((S,), F32) for _ in range(2)]
    B_invc = [Buf("invc0"), Buf("invc1")]
    d_invc = [sc.dsem("dinvc0"), sc.dsem("dinvc1")]
    pooledT = ar.alloc((2, 2, S), BF16)
    B_pooled = [[Buf(f"pooled{g_}{f_}") for f_ in range(2)] for g_ in range(2)]
    stg = [ar.alloc((S,), BF16) for _ in range(2)]
    B_stg = [Buf("stg0"), Buf("stg1")]
    d_stg = [sc.dsem("dstg0"), sc.dsem("dstg1")]
    pw = ar.alloc((4, 2, 256), BF16)
    B_pw = Buf("pw")
    d_pw = sc.dsem("dpw")
    sc.op("dve", lambda e: e.memset(upad[:, :, :, 0:8], 0.0), writes=[b_ for bb in B_up for b_ in bb])
    sc.op("dve", lambda e: e.memset(upad[:, :, :, 8 + S:16 + S], 0.0), writes=[b_ for bb in B_up for b_ in bb])
    abank = [0]
    stgc = [0]

    def inproj_group(slab, bs, fc, tt):
        bank = abank[0] % 4
        abank[0] += 1

        def fn(e):
            ins = None
            for kc in range(KC):
                ins = e.matmul(PS[bank][:, :], lhsT=slab[:, kc, fc * 128:(fc + 1) * 128], rhs=hT[:, kc, tt * 512:(tt + 1) * 512],
                               start=(kc == 0), stop=(kc == KC - 1))
            return ins
        sc.op("pe", fn, reads=[bs] + B_hTk, writes=[PB[bank]])
        return bank

    def w_in_slab(c0):
        r_ = load_w(w_in[:, c0:c0 + 256].rearrange("(kc p) n -> p kc n", p=128), 4096, v3(256))
        precast(1)
        return r_

    def pool_gen(gi):
        g2 = gi % 2
        L = S + 16
        slab, bs = w_in_slab(gi * 256)
        if gi == 1:
            for g_ in range(4):
                sc.dma("pool", d_pw, pw[:, g_], pool_w[g_].rearrange("(cc p) d -> p cc d", p=128), reads=[B_fence2], writes=[B_pw])
        sc.dma("sp", d_invc[g2], invc_sb[g2], invc[gi], writes=[B_invc[g2]])
        for fc in range(2):
            for tt in range(4):
                bank = inproj_group(slab, bs, fc, tt)
                sc.op("act", lambda e, bank=bank, fc=fc, tt=tt: e.activation(out=upad[:, g2, fc, 8 + tt * 512:8 + (tt + 1) * 512],
                                                                          in_=PS[bank][:, :], func=AF.Copy),
                      writes=[PB[bank], B_up[g2][fc]])
        yield
        for fc in range(2):
            up = upad[:, g2, fc, :]
            sc.op("dve", lambda e, up=up: e.tensor_tensor(out=sa[:, 1:L], in0=up[:, 0:L - 1], in1=up[:, 1:L], op=ALU.add),
                  reads=[B_up[g2][fc]], writes=[B_sa])
            cur, curB, oth, othB, lo = sa, B_sa, sb, B_sb, 1
            for sh in (1, 2, 4)[:gi]:
                nlo = lo + sh
                hi = L - nlo

                def f(e, cur=cur, oth=oth, nlo=nlo, hi=hi, sh=sh):
                    return e.tensor_tensor(out=oth[:, nlo:hi], in0=cur[:, nlo - sh:hi - sh], in1=cur[:, nlo + sh:hi + sh], op=ALU.add)
                sc.op("dve", f, reads=[curB], writes=[othB])
                cur, curB, oth, othB, lo = oth, othB, cur, curB, nlo
            sc.op("dve", lambda e, cur=cur, oth=oth: e.tensor_tensor(out=oth[:, 8:8 + S], in0=cur[:, 8:8 + S], in1=invc_sb[g2], op=ALU.mult),
                  reads=[curB, B_invc[g2]], writes=[othB])
            sc.op("dve", lambda e, oth=oth, up=up, fc=fc: e.tensor_tensor(out=pooledT[:, g2, fc, :], in0=oth[:, 8:8 + S], in1=up[:, 8:8 + S], op=ALU.subtract),
                  reads=[othB, B_up[g2][fc]], writes=[B_pooled[g2][fc]])
        yield
        for dc in range(2):
            si = stgc[0] % 2
            stgc[0] += 1
            for tt in range(4):
                bank = abank[0] % 4
                abank[0] += 1

                def fn(e, bank=bank, dc=dc, tt=tt):
                    ins = None
                    for cc in range(2):
                        ins = e.matmul(PS[bank][:, :], lhsT=pw[:, gi, cc, dc * 128:(dc + 1) * 128],
                                       rhs=pooledT[:, g2, cc, tt * 512:(tt + 1) * 512], start=(cc == 0), stop=(cc == 1))
                    return ins
                sc.op("pe", fn, reads=[B_pw] + B_pooled[g2], writes=[PB[bank]])
                col = gi * 2 + dc
                sc.op("act", lambda e, bank=bank, si=si, tt=tt, col=col: e.activation(out=stg[si][:, tt * 512:(tt + 1) * 512], in_=PS[bank][:, :],
                                                                                 func=AF.Copy, scale=psc_sb[:, col:col + 1]),
                      reads=[B_ident], writes=[PB[bank], B_stg[si]])
            sc.dma("sp", d_stg[si], catT[gi * 2 + dc], stg[si], reads=[B_stg[si]], writes=[B_cat[gi * 2 + dc]])

    run_pipeline([pool_gen(gi) for gi in range(4)], 3)

    if debug_stop == 2:
        sc.barrier()
        sc.replay()
        return nc
    ar.reset(m_p2)
    sc.barrier(skip=("pool",))

    junk3 = ar.alloc((DH,), BF16)
    cm = ar.alloc((514,), F32)
    dec_sb = ar.alloc((8,), F32)
    lg = ar.alloc((8,), F32)
    cdec = ar.alloc((8,), F32)
    kd = ar.alloc((8,), F32)
    DmT = ar.alloc((4, 128), F32)
    rowf = ar.alloc((4, 128), F32)
    rowb = ar.alloc((4, 128), F32)
    ttmp = ar.alloc((128,), F32)
    B_tab = Buf("tab")
    B_tt = Buf("ttmp")
    d_c = sc.dsem("dconst")
    sc.dma("sp", d_c, cm, cmat, writes=[B_tab])
    sc.dma("sp", d_c, dec_sb, decay, writes=[B_tab])
    sc.op("act", lambda e: e.activation(out=lg, in_=dec_sb, func=AF.Exp), reads=[B_tab], writes=[B_tt])
    sc.op("dve", lambda e: e.tensor_scalar(out=lg, in0=lg, scalar1=-1.0, scalar2=None, op0=ALU.mult), reads=[B_tt], writes=[B_tt])
    sc.op("act", lambda e: e.activation(out=cdec, in_=lg, func=AF.Exp, scale=128.0), reads=[B_tt], writes=[B_tab])
    Am, Bm, rF, rB = cm[:, 0:128], cm[:, 128:256], cm[:, 256:384], cm[:, 384:512]
    cF, cB = cm[:, 512:513], cm[:, 513:514]
    for h in range(NH):
        lf = lg[:, h:h + 1]
        lb = lg[:, 4 + h:5 + h]
        sc.op("dve", lambda e, lf=lf: e.tensor_scalar(out=ttmp, in0=Am, scalar1=lf, scalar2=LN16, op0=ALU.mult, op1=ALU.add),
              reads=[B_tab, B_tt], writes=[B_tt])
        sc.op("dve", lambda e, lb=lb: e.scalar_tensor_tensor(out=ttmp, in0=Bm, scalar=lb, in1=ttmp, op0=ALU.mult, op1=ALU.add),
              reads=[B_tab, B_tt], writes=[B_tt])
        sc.op("act", lambda e, h=h: e.activation(out=DmT[:, h, :], in_=ttmp, func=AF.Exp), reads=[B_tt], writes=[B_tt])
        sc.op("dve", lambda e, lf=lf: e.tensor_scalar(out=ttmp, in0=rF, scalar1=lf, scalar2=LN16, op0=ALU.mult, op1=ALU.add),
              reads=[B_tab, B_tt], writes=[B_tt])
        sc.op("act", lambda e, h=h: e.activation(out=rowf[:, h, :], in_=ttmp, func=AF.Exp), reads=[B_tt], writes=[B_tt])
        sc.op("dve", lambda e, lb=lb: e.tensor_scalar(out=ttmp, in0=rB, scalar1=lb, scalar2=LN16, op0=ALU.mult, op1=ALU.add),
              reads=[B_tab, B_tt], writes=[B_tt])
        sc.op("act", lambda e, h=h: e.activation(out=rowb[:, h, :], in_=ttmp, func=AF.Exp), reads=[B_tt], writes=[B_tt])
        sc.op("dve", lambda e, lf=lf: e.tensor_tensor(out=ttmp[:, 0:1], in0=cF, in1=lf, op=ALU.mult), reads=[B_tab, B_tt], writes=[B_tt])
        sc.op("act", lambda e, h=h: e.activation(out=kd[:, h:h + 1], in_=ttmp[:, 0:1], func=AF.Exp), reads=[B_tt], writes=[B_tt])
        sc.op("dve", lambda e, lb=lb: e.tensor_tensor(out=ttmp[:, 0:1], in0=cB, in1=lb, op=ALU.mult), reads=[B_tab, B_tt], writes=[B_tt])
        sc.op("act", lambda e, h=h: e.activation(out=kd[:, 4 + h:5 + h], in_=ttmp[:, 0:1], func=AF.Exp), reads=[B_tt], writes=[B_tt])
    B_T = B_tt

    qT = ar.alloc((2, S), BF16)
    kT = ar.alloc((2, S), BF16)
    vtok = ar.alloc((NCH, DH), BF16)
    sgT = ar.alloc((2, S), BF16)
    Sb_all = ar.alloc((NCH, 512), BF16)
    Sf_all = ar.alloc((NCH, 512), BF16)
    rstg = [ar.alloc((2, 128), BF16) for _ in range(4)]
    B_qT = [Buf(f"qT{t}") for t in range(4)]
    B_kT = [Buf(f"kT{t}") for t in range(4)]
    B_v = [Buf(f"v{n}") for n in range(NCH)]
    B_sg = [Buf(f"sg{t}") for t in range(4)]
    B_Sb = [Buf(f"Sb{n}") for n in range(NCH)]
    B_Sf = [Buf(f"Sf{n}") for n in range(NCH)]
    B_rstg = [Buf(f"rstg{n}") for n in range(4)]
    d_rstg = [sc.dsem(f"drstg{n}") for n in range(4)]
    csb = [ar.alloc((2, 256), F32) for _ in range(2)]
    B_cs = [Buf("cs0"), Buf("cs1")]
    d_cs = [sc.dsem("dcs0"), sc.dsem("dcs1")]
    rt = [ar.alloc((512,), F32) for _ in range(2)]
    B_rt = [Buf(f"rt{i}") for i in range(2)]
    S32 = {d_: [ar.alloc((512,), F32) for _ in range(2)] for d_ in "fb"}
    B_S32 = {d_: [Buf(f"S32{d_}0"), Buf(f"S32{d_}1")] for d_ in "fb"}
    NKR = 4
    kdt = [ar.alloc((DH,), BF16) for _ in range(NKR)]
    B_kdt = [Buf(f"kdt{i}") for i in range(NKR)]
    PTm = [ar.alloc((128,), BF16) for _ in range(2)]
    B_PTm = [Buf("PTm0"), Buf("PTm1")]
    qf = [ar.alloc((2, 128), BF16) for _ in range(2)]
    qb = [ar.alloc((2, 128), BF16) for _ in range(2)]
    B_qf = [Buf("qf0"), Buf("qf1")]
    B_qb = [Buf("qb0"), Buf("qb1")]
    retn = [ar.alloc((DH,), BF16) for _ in range(2)]
    B_retn = [Buf("retn0"), Buf("retn1")]
    PTb7 = PS[7][:, :].bitcast(BF16)
    PTb3 = PS[3][:, :].bitcast(BF16)
    csc = [0]
    kdc = [0]
    dlc = [0]
    sc.op("dve", lambda e: e.memset(Sb_all[:, NCH - 1, :], 0.0), writes=[B_Sb[NCH - 1]])
    sc.op("dve", lambda e: e.memset(Sf_all[:, 0, :], 0.0), writes=[B_Sf[0]])

    def rope_evac(dstT, dstB, tt, b0, b1):
        for hf in range(2):
            i = csc[0] % 2
            csc[0] += 1
            c0 = tt * 512 + hf * 256
            sc.dma("sp", d_cs[i], csb[i], cs[:, :, c0:c0 + 256], writes=[B_cs[i]])
            co, si = csb[i][:, 0, :], csb[i][:, 1, :]
            p0, p1 = PS[b0][:, hf * 256:(hf + 1) * 256], PS[b1][:, hf * 256:(hf + 1) * 256]
            sl = slice(c0, c0 + 256)
            r0, r1 = rt[0][:, 0:256], rt[1][:, 0:256]

            def ops(i=i, co=co, si=si, p0=p0, p1=p1, sl=sl, r0=r0, r1=r1):
                sc.op("dve", lambda e: e.tensor_tensor(out=r0, in0=p0, in1=co, op=ALU.mult), reads=[B_cs[i]], writes=[PB[b0], B_rt[0]])
                sc.op("dve", lambda e: e.tensor_tensor(out=r1, in0=p1, in1=si, op=ALU.mult), reads=[B_cs[i]], writes=[PB[b1], B_rt[1]])
                sc.op("dve", lambda e: e.tensor_tensor(out=dstT[:, 0, sl], in0=r0, in1=r1, op=ALU.subtract),
                      reads=[B_rt[0], B_rt[1]], writes=[dstB[tt]])
                sc.op("dve", lambda e: e.tensor_tensor(out=r0, in0=p0, in1=si, op=ALU.mult), reads=[B_cs[i]], writes=[PB[b0], B_rt[0]])
                sc.op("dve", lambda e: e.tensor_tensor(out=r1, in0=p1, in1=co, op=ALU.mult), reads=[B_cs[i]], writes=[PB[b1], B_rt[1]])
                sc.op("dve", lambda e: e.tensor_tensor(out=dstT[:, 1, sl], in0=r0, in1=r1, op=ALU.add),
                      reads=[B_rt[0], B_rt[1]], writes=[dstB[tt]])
            ops()

    def k_tok(n, col):
        j = kdc[0] % NKR
        tb, ptb = (7, PTb7) if kdc[0] % 2 == 0 else (3, PTb3)
        kdc[0] += 1

        def fn(e):
            ins = None
            for fc in range(2):
                ins = e.transpose(out=ptb[:, fc * 128:(fc + 1) * 128], in_=kT[:, fc, n * 128:(n + 1) * 128], identity=ident)
            return ins
        sc.op("pe", fn, reads=[B_kT[n // 4], B_small], writes=[PB[tb]])
        sc.op("act", lambda e: e.activation(out=kdt[j], in_=ptb[:, 0:256], func=AF.Copy, scale=kd[:, col:col + 1]),
              reads=[B_T], writes=[PB[tb], B_kdt[j]])
        return kdt[j], B_kdt[j]

    def state_delta(kt, kB, n):
        bank = 6 if dlc[0] % 2 == 0 else 2
        dlc[0] += 1

        def fn(e):
            ins = None
            for dc in range(2):
                ins = e.matmul(PS[bank][:, dc * 256:(dc + 1) * 256], lhsT=kt[:, dc * 128:(dc + 1) * 128], rhs=vtok[:, n, :],
                               start=True, stop=True)
            return ins
        sc.op("pe", fn, reads=[kB, B_v[n]], writes=[PB[bank]])
        return bank

    def chain_gen(h, dirn, n, st):
        col = h if dirn == "f" else 4 + h
        kt, kB = k_tok(n, col)
        yield
        bank = state_delta(kt, kB, n)
        yield
        cur = st[dirn]
        nxt = 1 - cur
        st[dirn] = nxt
        sc.op("dve", lambda e: e.scalar_tensor_tensor(out=S32[dirn][nxt], in0=S32[dirn][cur], scalar=cdec[:, col:col + 1], in1=PS[bank][:, :],
                                                     op0=ALU.mult, op1=ALU.add),
              reads=[B_S32[dirn][cur], B_tab], writes=[PB[bank], B_S32[dirn][nxt]])
        if dirn == "f":
            sc.op("act", lambda e: e.activation(out=Sf_all[:, n + 1, :], in_=S32[dirn][nxt], func=AF.Copy),
                  reads=[B_S32[dirn][nxt]], writes=[B_Sf[n + 1]])
        else:
            sc.op("act", lambda e: e.activation(out=Sb_all[:, n - 1, :], in_=S32[dirn][nxt], func=AF.Copy),
                  reads=[B_S32[dirn][nxt]], writes=[B_Sb[n - 1]])

    def out_gen(h, n):
        p = n % 2
        ns = slice(n * 128, (n + 1) * 128)
        tq = n // 4
        sbank = (4, 0)[p]
        obank = (5, 1)[p]
        tbank, ptb = ((7, PTb7), (3, PTb3))[p]

        def fsc(e):
            ins = None
            for dc in range(2):
                ins = e.matmul(PS[sbank][:, 0:128], lhsT=kT[:, dc, ns], rhs=qT[:, dc, ns], start=(dc == 0), stop=(dc == 1))
            return ins
        sc.op("pe", fsc, reads=[B_kT[tq], B_qT[tq]], writes=[PB[sbank]])
        sc.op("dve", lambda e: e.tensor_tensor(out=PTm[p], in0=PS[sbank][:, 0:128], in1=DmT[:, h, :], op=ALU.mult),
              reads=[B_T], writes=[PB[sbank], B_PTm[p]])
        sc.op("dve", lambda e: e.tensor_tensor(out=qf[p], in0=qT[:, :, ns], in1=rowf[:, h:h + 1, :].to_broadcast([128, 2, 128]), op=ALU.mult),
              reads=[B_qT[tq], B_T], writes=[B_qf[p]])
        sc.op("dve", lambda e: e.tensor_tensor(out=qb[p], in0=qT[:, :, ns], in1=rowb[:, h:h + 1, :].to_broadcast([128, 2, 128]), op=ALU.mult),
              reads=[B_qT[tq], B_T], writes=[B_qb[p]])
        yield

        def fo(e):
            e.matmul(PS[obank][:, 0:256], lhsT=PTm[p], rhs=vtok[:, n, :], start=True, stop=False)
            for dc in range(2):
                e.matmul(PS[obank][:, 0:256], lhsT=qf[p][:, dc, :], rhs=Sf_all[:, n, dc * 256:(dc + 1) * 256], start=False, stop=False)
            ins = None
            for dc in range(2):
                ins = e.matmul(PS[obank][:, 0:256], lhsT=qb[p][:, dc, :], rhs=Sb_all[:, n, dc * 256:(dc + 1) * 256], start=False, stop=(dc == 1))
            return ins
        sc.op("pe", fo, reads=[B_PTm[p], B_v[n], B_qf[p], B_qb[p], B_Sf[n], B_Sb[n]], writes=[PB[obank]])
        yield
        rs, Bs = rms_rstd(PS[obank][:, 0:256], [], DH, excl=[PB[obank]], jbuf=junk3)
        sc.op("act", lambda e: e.activation(out=retn[p], in_=PS[obank][:, 0:256], func=AF.Copy, scale=rs),
              reads=[Bs], writes=[PB[obank], B_retn[p]])
        yield

        def ftr(e):
            ins = None
            for ec in range(2):
                ins = e.transpose(out=ptb[:, 512 + ec * 128:512 + (ec + 1) * 128], in_=retn[p][:, ec * 128:(ec + 1) * 128], identity=ident)
            return ins
        sc.op("pe", ftr, reads=[B_retn[p], B_small], writes=[PB[tbank]])
        r4 = n % 4
        sc.op("dve", lambda e: e.tensor_tensor(out=rstg[r4], in0=ptb[:, 512:768].rearrange("p (a b) -> p a b", b=128),
                                              in1=sgT[:, :, ns], op=ALU.mult),
              reads=[B_sg[tq]], writes=[PB[tbank], B_rstg[r4]])
        sc.dma("sp", d_rstg[r4], catT[8 + 2 * h:10 + 2 * h, :, ns].rearrange("c p t -> p c t"), rstg[r4],
               reads=[B_rstg[r4]], writes=[B_cat[8 + 2 * h], B_cat[9 + 2 * h]])

    B_pp2 = Buf("pp2")
    B_ab2 = Buf("ab2")
    for h in range(NH):
        if h == 2:
            load_pp(b2, 3, B_pp2)
        if h == 3:
            load_pp(sc2_pp, 4, B_pp2)
            sc.op("dve", lambda e: e.scalar_tensor_tensor(out=a2, in0=sc2_pp, scalar=1.0, in1=g2n_sb, op0=ALU.add, op1=ALU.mult),
                  reads=[B_pp2, B_ident], writes=[B_ab2])
        mg = mod_gen(2 + h, 6)
        for which, dstT, dstB in ((0, qT, B_qT), (1, kT, B_kT)):
            slab, bs = w_in_slab(1024 + which * 1024 + h * 256)
            for tt in range(4):
                b0 = inproj_group(slab, bs, 0, tt)
                b1_ = inproj_group(slab, bs, 1, tt)
                rope_evac(dstT, dstB, tt, b0, b1_)
            next(mg)
            next(mg)
        slab, bs = w_in_slab(3072 + h * 256)
        for n2 in range(NCH // 2):
            bank = abank[0] % 4
            abank[0] += 1

            def fn(e, slab=slab, n2=n2, bank=bank):
                ins = None
                for j in range(2):
                    n = n2 * 2 + j
                    for kc in range(KC):
                        ins = e.matmul(PS[bank][:, j * 256:(j + 1) * 256], lhsT=hT[:, kc, n * 128:(n + 1) * 128], rhs=slab[:, kc, :],
                                       start=(kc == 0), stop=(kc == KC - 1))
                return ins
            sc.op("pe", fn, reads=[bs] + B_hTk, writes=[PB[bank]])
            sc.op("act", lambda e, bank=bank, n2=n2: e.activation(out=vtok[:, n2 * 2:n2 * 2 + 2, :],
                                                                 in_=PS[bank][:, :].rearrange("p (a b) -> p a b", b=256), func=AF.Copy),
                  writes=[PB[bank], B_v[n2 * 2], B_v[n2 * 2 + 1]])
        next(mg)
        next(mg)
        slab, bs = w_in_slab(4096 + h * 256)
        for fc in range(2):
            for tt in range(4):
                bank = inproj_group(slab, bs, fc, tt)
                sc.op("act", lambda e, bank=bank, fc=fc, tt=tt: e.activation(out=sgT[:, fc, tt * 512:(tt + 1) * 512], in_=PS[bank][:, :], func=AF.Silu),
                      writes=[PB[bank], B_sg[tt]])

        next(mg)
        next(mg)
        st = {"f": 0, "b": 0}
        sc.op("dve", lambda e: e.memset(S32["f"][0], 0.0), writes=[B_S32["f"][0]])
        sc.op("dve", lambda e: e.memset(S32["b"][0], 0.0), writes=[B_S32["b"][0]])
        gens = []
        for i in range(NCH - 1):
            gens.append(chain_gen(h, "f", i, st))
            gens.append(chain_gen(h, "b", NCH - 1 - i, st))
        run_pipeline(gens, 6)
        run_pipeline([out_gen(h, n) for n in range(NCH)], 4)

    peak3 = ar.off
    if debug_stop == 3:
        sc.barrier()
        sc.replay()
        return nc
    precast(len(pre_jobs))
    for b_ in B_wsc:
        b_.w = (d_pre.h, d_pre.val)
    ar.reset(m_persist)
    sc.barrier(skip=("pool",))
    B_fence4 = Buf("fence4")
    tok_f4 = sc.op("dve", lambda e: e.memset(stat[:, 7:8], 0.0), writes=[B_fence4])

    g1_bc = ar.alloc((D,), F32)
    g2_bc = ar.alloc((D,), F32)
    fg_bc = ar.alloc((D,), F32)
    B_bc = Buf("bc")
    B_bc2 = Buf("bc2")
    d_bc = sc.dsem("dbc")
    d_bc2 = sc.dsem("dbc2")
    x1 = ar.alloc((4, D), F32)
    B_x1 = [Buf(f"x1_{m}") for m in range(4)]
    d_x1 = [sc.dsem(f"dx1_{m}") for m in range(4)]
    d_o = [sc.dsem(f"dout{m}") for m in range(4)]
    ctile = ar.alloc((KC, 512), BF16)
    B_c = Buf("ctile")
    B_h2 = [[Buf(f"h2_{m}_{hf}") for hf in range(2)] for m in range(4)]
    B_h2k = [Buf(f"h2k{kc}") for kc in range(KC)]
    B_h2all = [b_ for bb in B_h2 for b_ in bb] + B_h2k
    d_ct = sc.dsem("dct")
    actT = ar.alloc((NFC, 512), BF16)
    B_act = [Buf(f"act{j}") for j in range(NFC)]
    sgt = [ar.alloc((512,), F32) for _ in range(2)]
    B_sgt = [Buf("sgt0"), Buf("sgt1")]
    tmpA = [ar.alloc((512,), F32) for _ in range(2)]
    B_tmpA = [Buf("tmpA0"), Buf("tmpA1")]
    while ar.off + 8192 + 64 <= ar.n:
        wslot.append(ar.alloc((4096,), BF16))
        B_ws.append(Buf(f"ws{len(B_ws)}"))
        B_ws[-1].w = tok_f4
        d_ws.append(sc.dsem(f"dws{len(d_ws)}"))
    ring4 = [0, 1, 2] + list(range(5, len(wslot)))
    slot_ctr[0] = 0

    def load_w4(src_ap, ncols_total, view, sid=None):
        i = ring4[slot_ctr[0] % len(ring4)]
        slot_ctr[0] += 1
        flat = wslot[i][:, 0:ncols_total]
        dst = view(flat)
        if sid is None:
            sc.dma("pool", d_ws[i], dst, src_ap, writes=[B_ws[i]])
        else:
            sc.dma("pool", d_ws[i], flat, wsc[sid][:, 0:ncols_total], reads=[B_wsc[sid]], writes=[B_ws[i]])
        return dst, B_ws[i]

    tctr = [0]
    def load_ctile(tt):
        sc.dma("sp", d_ct, ctile, catT[:, :, tt * 512:(tt + 1) * 512].rearrange("kc p t -> p kc t"), reads=B_cat, writes=[B_c] + B_h2all)
    load_ctile(0)
    for m in range(4):
        sc.dma("sp", d_x1[m], x1[:, m, :], x[m * 128:(m + 1) * 128, :], writes=[B_x1[m]])
    sc.dma("sp", d_bc, g1_bc, modrows[2].partition_broadcast(128), reads=[B_mod[2]], writes=[B_bc])
    sc.dma("sp", d_bc2, g2_bc, modrows[5].partition_broadcast(128), reads=[B_mod[5]], writes=[B_bc2])
    sc.dma("sp", d_bc2, fg_bc, fgr.partition_broadcast(128), writes=[B_bc2])
    for tt in range(4):
        for m in range(4):
            r0 = (tt * 4 + m) * 128
            if tt > 0:
                sc.dma("sp", d_x1[m], x1[:, m, :], x[r0:r0 + 128, :], writes=[B_x1[m]])
        ob = 0
        for db in range(4):
            ds_ = slice(db * 512, (db + 1) * 512)
            halves = []
            for hf in range(2):
                halves.append(load_w4(None, 4096, v3(512), sid=db * 2 + hf))
            for m in range(4):
                bank = ob % 8
                ob += 1

                def fn(e, halves=halves, m=m, bank=bank):
                    ins = None
                    for kc in range(KC):
                        slab = halves[kc // 8][0]
                        ins = e.matmul(PS[bank][:, :], lhsT=ctile[:, kc, m * 128:(m + 1) * 128], rhs=slab[:, kc % 8, :],
                                       start=(kc == 0), stop=(kc == KC - 1))
                    return ins
                sc.op("pe", fn, reads=[halves[0][1], halves[1][1], B_c], writes=[PB[bank]])
                ti = tctr[0] % 2
                tctr[0] += 1
                sc.op("dve", lambda e, bank=bank, ti=ti, ds_=ds_: e.tensor_tensor(out=tmpA[ti], in0=PS[bank][:, :], in1=g1_bc[:, ds_], op=ALU.mult),
                      reads=[B_bc], writes=[PB[bank], B_tmpA[ti]])
                sc.op("dve", lambda e, m=m, ti=ti, ds_=ds_: e.tensor_tensor(out=x1[:, m, ds_], in0=x1[:, m, ds_], in1=tmpA[ti], op=ALU.add),
                      reads=[B_tmpA[ti]], writes=[B_x1[m]])
        sc.op("dve", lambda e: e.memset(stat[:, 3:4], 0.0), writes=[B_c] + B_h2all)
        run_pipeline([nt_gen(x1[:, m, :], B_x1[m], ctile, B_h2[m], m, m) for m in range(4)], 2)
        modulate(ctile, a2, b2, B_ab2, [B_h2[m][0] for m in range(4)], [B_h2[m][1] for m in range(4)], B_h2k, ["dve", "act"])
        for jp in range(NFC // 2):
            cs_ = slice(jp * 256, (jp + 1) * 256)
            gsl, gB = load_w4(w_gate[:, cs_].rearrange("(kc p) n -> p kc n", p=128), 4096, v3(256))
            usl, uB = load_w4(w_up[:, cs_].rearrange("(kc p) n -> p kc n", p=128), 4096, v3(256))
            for jj in range(2):
                j = jp * 2 + jj
                gb, ub = 4 + j % 2, 6 + j % 2
                for slab, sB, bank in ((gsl, gB, gb), (usl, uB, ub)):
                    def fn(e, slab=slab, bank=bank, jj=jj):
                        ins = None
                        for kc in range(KC):
                            ins = e.matmul(PS[bank][:, :], lhsT=slab[:, kc, jj * 128:(jj + 1) * 128], rhs=ctile[:, kc, :],
                                           start=(kc == 0), stop=(kc == KC - 1))
                        return ins
                    sc.op("pe", fn, reads=[sB] + B_h2k, writes=[PB[bank]])
                si = j % 2
                sc.op("act", lambda e, si=si, gb=gb: e.activation(out=sgt[si], in_=PS[gb][:, :], func=AF.Silu),
                      writes=[PB[gb], B_sgt[si]])
                sc.op("dve", lambda e, si=si, ub=ub, j=j: e.tensor_tensor(out=actT[:, j, :], in0=PS[ub][:, :], in1=sgt[si], op=ALU.mult),
                      reads=[B_sgt[si]], writes=[PB[ub], B_act[j]])
        if tt < 3:
            load_ctile(tt + 1)
        for db in range(4):
            ds_ = slice(db * 512, (db + 1) * 512)
            bo = 4 * (db % 2)
            for jg in range(NFC // 4):
                slab, sB = load_w4(None, 2048, v3(512), sid=30 + db * 11 + jg)

                def fn(e, slab=slab, jg=jg, bo=bo):
                    ins = None
                    for a in range(4):
                        j = jg * 4 + a
                        for m in range(4):
                            ins = e.matmul(PS[bo + m][:, :], lhsT=actT[:, j, m * 128:(m + 1) * 128], rhs=slab[:, a, :],
                                           start=(j == 0), stop=(j == NFC - 1))
                    return ins
                sc.op("pe", fn, reads=[sB] + B_act[jg * 4:(jg + 1) * 4], writes=PB[bo:bo + 4])
            for m in range(4):
                ti = tctr[0] % 2
                tctr[0] += 1
                sc.op("dve", lambda e, m=m, ti=ti, ds_=ds_, bo=bo: e.tensor_tensor(out=tmpA[ti], in0=PS[bo + m][:, :], in1=g2_bc[:, ds_], op=ALU.mult),
                      reads=[B_bc2], writes=[PB[bo + m], B_tmpA[ti]])
                sc.op("dve", lambda e, m=m, ti=ti, ds_=ds_: e.tensor_tensor(out=x1[:, m, ds_], in0=x1[:, m, ds_], in1=tmpA[ti], op=ALU.add),
                      reads=[B_tmpA[ti]], writes=[B_x1[m]])
        for m in range(4):
            rs, Bs = rms_rstd(x1[:, m, :], [B_x1[m]], D)
            sc.op("dve", lambda e, m=m, rs=rs: e.scalar_tensor_tensor(out=x1[:, m, :], in0=x1[:, m, :], scalar=rs, in1=fg_bc, op0=ALU.mult, op1=ALU.mult),
                  reads=[Bs, B_bc2], writes=[B_x1[m]])
            r0 = (tt * 4 + m) * 128
            sc.dma("sp", d_o[m], out[r0:r0 + 128, :], x1[:, m, :], reads=[B_x1[m]])

    sc.barrier()
    sc.replay()
    print("arena peaks", peak3, ar.off, "ops", sc.cnt, "sems", len(sc.dsems) + 5)
    return nc


def _consts():
    half = 128
    inv = (1.0 / (np.float32(10000.0) ** np.linspace(0.0, 1.0, half, dtype=np.float32))).astype(np.float32)
    ang = (np.arange(S, dtype=np.float32)[:, None] * inv[None, :]).astype(np.float32)
    cs = np.stack([np.cos(ang).T, np.sin(ang).T], axis=1).astype(np.float32)
    j = np.arange(128)[:, None]
    i = np.arange(128)[None, :]
    cm = np.zeros((128, 514), np.float32)
    cm[:, 0:128] = np.maximum(i - j, 0)
    cm[:, 128:256] = np.maximum(j - i, 0)
    cm[:, 256:384] = (i + 1)
    cm[:, 384:512] = (128 - i)
    cm[:, 512] = 127 - np.arange(128)
    cm[:, 513] = np.arange(128)
    t = np.arange(S)
    invc = np.zeros((4, 128, S), np.float32)
    for gi, w in enumerate((2, 4, 8, 16)):
        lo = np.clip(t - w // 2, 0, S)
        hi = np.clip(t + w // 2, 0, S)
        invc[gi] = (1.0 / (hi - lo).astype(np.float32))[None, :]
    return cs, cm, invc, np.eye(128, dtype=np.float32)


_NC_CACHE = {}


def kernel(x, c, w_ada, b_ada, norm1_g, w_in, pool_w, pool_scale, ret_decay_fwd, ret_decay_bwd,
           w_out, norm2_g, w_gate, w_up, w_down, final_g):
    f = lambda a: np.ascontiguousarray(np.asarray(a, dtype=np.float32))
    x, c = f(x), f(c)
    cs, cm, invc, identf = _consts()
    pp = lambda v: np.ascontiguousarray(v.reshape(-1, 128).T)
    shared = {
        "w_ada": f(w_ada)[0], "b_ada": f(b_ada)[0].reshape(1, -1),
        "g1n": pp(f(norm1_g)[0]), "g2n": pp(f(norm2_g)[0]), "fg": f(final_g),
        "g1row": f(norm1_g)[0], "g2row": f(norm2_g)[0],
        "w_in": f(w_in)[0], "pool_w": f(pool_w)[0], "pscale": pp(f(pool_scale)[0]),
        "decay": np.ascontiguousarray(np.broadcast_to(
            np.concatenate([f(ret_decay_fwd)[0], f(ret_decay_bwd)[0]])[None, :], (128, 8))),
        "w_out": f(w_out)[0], "w_gate": f(w_gate)[0], "w_up": f(w_up)[0], "w_down": f(w_down)[0],
        "cs": cs, "cmat": cm, "invc": invc, "identf": identf,
    }
    if "nc" not in _NC_CACHE:
        _NC_CACHE["nc"] = build_nc()
    nc = _NC_CACHE["nc"]
    in_maps = []
    for b in range(NCORES):
        m = dict(shared)
        m["x"] = x[b]
        m["cT"] = pp(c[b])
        in_maps.append(m)
    res = run_bass_kernel_spmd(nc, in_maps, core_ids=list(range(NCORES)))
    return np.stack([np.asarray(r["out"], dtype=np.float32) for r in res.results], axis=0)
```

```python
import numpy as np
import concourse.bass as bass
import concourse.mybir as mybir
from concourse.bass_utils import run_bass_kernel_spmd

F32 = mybir.dt.float32
BF16 = mybir.dt.bfloat16
U8 = mybir.dt.uint8
ALU = mybir.AluOpType
AF = mybir.ActivationFunctionType

S = 2048
D = 2048
KC = 16
DFF = 5632
NFC = 44
INW = 5120
CH = 128
NCH = 16
NH = 4
DH = 256
EPS = 1e-6
NCORES = 8


class Buf:
    def __init__(self, name):
        self.name = name
        self.w = None
        self.r = {}


class DSem:
    def __init__(self, nc, name):
        self.h = nc.alloc_semaphore(name)
        self.val = 0


class Sched:
    ENG = ("pe", "act", "dve", "pool", "sp")

    def __init__(self, nc):
        self.nc = nc
        self.sem = {e: nc.alloc_semaphore("eng_" + e) for e in self.ENG}
        self.cnt = {e: 0 for e in self.ENG}
        self.seen = {e: {} for e in self.ENG}
        self.prog = {e: [] for e in self.ENG}
        self.dsems = []

    def dsem(self, name):
        d = DSem(self.nc, name)
        self.dsems.append(d)
        return d

    def _deps(self, e, reads, writes):
        need = {}

        def add(tok):
            if tok is None:
                return
            s, v = tok
            k = id(s)
            if k not in need or need[k][1] < v:
                need[k] = (s, v)

        for b in reads:
            add(b.w)
        for b in writes:
            add(b.w)
            for t in b.r.values():
                add(t)
        out = []
        for k, (s, v) in need.items():
            if e == "pe" and s is self.sem["pe"]:
                continue
            if self.seen[e].get(k, 0) >= v:
                continue
            self.seen[e][k] = v
            out.append((s, v))
        return out

    def _mark(self, tok, reads, writes):
        for b in reads:
            b.r[id(tok[0])] = tok
        for b in writes:
            b.w = tok
            b.r = {}

    def op(self, e, fn, reads=(), writes=()):
        waits = self._deps(e, reads, writes)
        self.cnt[e] += 1
        tok = (self.sem[e], self.cnt[e])

        def emit(eng, waits=waits, fn=fn, s=tok[0]):
            for ws, wv in waits:
                eng.wait_ge(ws, wv)
            fn(eng).then_inc(s, 1)

        self.prog[e].append(emit)
        self._mark(tok, reads, writes)
        return tok

    def dma(self, q, dsem, out_ap, in_ap, reads=(), writes=(), **kw):
        waits = self._deps(q, reads, writes)
        dsem.val += 16
        tok = (dsem.h, dsem.val)

        def emit(eng, waits=waits, s=dsem.h):
            for ws, wv in waits:
                eng.wait_ge(ws, wv)
            eng.dma_start(out=out_ap, in_=in_ap, **kw).then_inc(s, 16)

        self.prog[q].append(emit)
        self._mark(tok, reads, writes)
        return tok

    def barrier(self, skip=(), skip_dsems=()):
        toks = [(self.sem[e], self.cnt[e]) for e in self.ENG if self.cnt[e] > 0]
        toks += [(d.h, d.val) for d in self.dsems if d.val > 0 and d not in skip_dsems]
        for e in self.ENG:
            if e in skip:
                continue
            waits = []
            for s, v in toks:
                k = id(s)
                if self.seen[e].get(k, 0) >= v:
                    continue
                if e == "pe" and s is self.sem["pe"]:
                    continue
                self.seen[e][k] = v
                waits.append((s, v))

            def emit(eng, waits=waits):
                for ws, wv in waits:
                    eng.wait_ge(ws, wv)

            self.prog[e].append(emit)

    def replay(self):
        nc = self.nc
        with nc.Block() as blk:
            @blk.tensor
            def _(e):
                for f in self.prog["pe"]:
                    f(e)

            @blk.scalar
            def _(e):
                for f in self.prog["act"]:
                    f(e)

            @blk.vector
            def _(e):
                for f in self.prog["dve"]:
                    f(e)

            @blk.gpsimd
            def _(e):
                for f in self.prog["pool"]:
                    f(e)

            @blk.sync
            def _(e):
                for f in self.prog["sp"]:
                    f(e)


class Arena:
    def __init__(self, nc, nbytes):
        self.t = nc.alloc_sbuf_tensor("arena", [128, nbytes], U8)
        self.n = nbytes
        self.off = 0

    def mark(self):
        return self.off

    def reset(self, m):
        self.off = m

    def alloc(self, shape, dtype, parts=128):
        esz = 2 if dtype == BF16 else 4
        n = int(np.prod(shape)) * esz
        n = (n + 63) // 64 * 64
        assert self.off + n <= self.n, f"SBUF arena overflow {self.off}+{n}>{self.n}"
        ap = self.t[0:parts, self.off:self.off + n]
        self.off += n
        ap = ap.bitcast(dtype)
        ne = int(np.prod(shape))
        ap = ap[:, 0:ne]
        if len(shape) == 2:
            ap = ap.rearrange("p (a b) -> p a b", b=shape[1])
        elif len(shape) == 3:
            ap = ap.rearrange("p (a b c) -> p a b c", b=shape[1], c=shape[2])
        return ap


def run_pipeline(gens, window):
    for _ in pipeline_rounds(gens, window):
        pass


def pipeline_rounds(gens, window):
    active = []
    it = iter(gens)
    pending = True
    while pending or active:
        if pending and len(active) < window:
            try:
                active.append(next(it))
            except StopIteration:
                pending = False
        for g in reversed(list(active)):
            try:
                next(g)
            except StopIteration:
                active.remove(g)
        yield


LN16 = -2.772588722239781


def build_nc(debug_stop=None):
    nc = bass.Bass("TRN2", target_bir_lowering=False)
    dbg_kind = "ExternalOutput" if debug_stop else "Internal"

    def din(name, shape, dt=F32):
        return nc.dram_tensor(name, shape, dt, kind="ExternalInput").ap()

    x = din("x", [S, D])
    cT = din("cT", [128, KC])
    w_ada = din("w_ada", [D, 6 * D])
    b_ada = din("b_ada", [1, 6 * D])
    g1n = din("g1n", [128, KC])
    g2n = din("g2n", [128, KC])
    fgr = din("fg", [D])
    g1row = din("g1row", [D])
    g2row = din("g2row", [D])
    w_in = din("w_in", [D, INW])
    pool_w = din("pool_w", [4, 256, 256])
    pscale = din("pscale", [128, 8])
    decay = din("decay", [128, 8])
    w_out = din("w_out", [D, D])
    w_gate = din("w_gate", [D, DFF])
    w_up = din("w_up", [D, DFF])
    w_down = din("w_down", [DFF, D])
    cs = din("cs", [128, 2, S])
    cmat = din("cmat", [128, 514])
    invc = din("invc", [4, 128, S])
    identf = din("identf", [128, 128])
    out = nc.dram_tensor("out", [S, D], F32, kind="ExternalOutput").ap()
    catT = nc.dram_tensor("catT", [16, 128, S], BF16, kind=dbg_kind).ap()
    modrows = nc.dram_tensor("modrows", [6, D], F32, kind=dbg_kind).ap()

    sc = Sched(nc)
    ar = Arena(nc, 207 * 1024)
    PS = [nc.alloc_psum_tensor(f"ps{i}", [128, 512], F32) for i in range(8)]
    PB = [Buf(f"psb{i}") for i in range(8)]

    NSC = 8 + 22 + 44
    wsc = nc.dram_tensor("wsc", [NSC, 128, 4096], BF16).ap()
    B_wsc = [Buf(f"wsc{i}") for i in range(NSC)]
    d_pre = sc.dsem("dprecast")
    pre_jobs = []
    for db in range(4):
        for hf in range(2):
            pre_jobs.append((db * 2 + hf, wsc[db * 2 + hf][:, 0:4096].rearrange("p (kc n) -> p kc n", n=512),
                             w_out[hf * 1024:(hf + 1) * 1024, db * 512:(db + 1) * 512].rearrange("(kc p) n -> p kc n", p=128)))
    for db in range(4):
        for jg in range(11):
            sid = 30 + db * 11 + jg
            pre_jobs.append((sid, wsc[sid][:, 0:2048].rearrange("p (a n) -> p a n", n=512),
                             w_down[jg * 512:(jg + 1) * 512, db * 512:(db + 1) * 512].rearrange("(a p) n -> p a n", p=128)))
    pre_it = iter(pre_jobs)

    def precast(n):
        for _ in range(n):
            job = next(pre_it, None)
            if job is None:
                return
            sid, dst, src = job
            sc.dma("pool", d_pre, dst, src, writes=[B_wsc[sid]])

    ident = ar.alloc((128,), BF16)
    B_ident = Buf("ident")
    epsb = ar.alloc((1,), F32)
    a1 = ar.alloc((KC,), F32)
    b1 = ar.alloc((KC,), F32)
    a2 = ar.alloc((KC,), F32)
    b2 = ar.alloc((KC,), F32)
    g1n_sb = ar.alloc((KC,), F32)
    g2n_sb = ar.alloc((KC,), F32)
    psc_sb = ar.alloc((8,), F32)
    cact = ar.alloc((KC,), BF16)
    sc2_pp = ar.alloc((KC,), F32)
    B_small = Buf("small")
    B_a1, B_a2 = Buf("a1"), Buf("a2")
    B_cact = Buf("cact")
    B_mod = [Buf(f"modrow{v}") for v in range(6)]
    B_cat = [Buf(f"cat{i}") for i in range(16)]
    d_mod = [sc.dsem(f"dmod{v}") for v in range(6)]
    d_misc = sc.dsem("dmisc")
    d_misc2 = sc.dsem("dmisc2")

    NSLOT = 3
    wslot = [ar.alloc((4096,), BF16) for _ in range(NSLOT)]
    B_ws = [Buf(f"ws{i}") for i in range(NSLOT)]
    d_ws = [sc.dsem(f"dws{i}") for i in range(NSLOT)]
    slot_ctr = [0]

    def load_w(src_ap, ncols_total, view):
        i = slot_ctr[0] % len(wslot)
        slot_ctr[0] += 1
        dst = wslot[i][:, 0:ncols_total]
        dst = view(dst)
        sc.dma("pool", d_ws[i], dst, src_ap, writes=[B_ws[i]])
        return dst, B_ws[i]

    def v3(b):
        return lambda ap: ap.rearrange("p (a b) -> p a b", b=b)

    tmpf = ar.alloc((128,), F32)
    cT_sb = ar.alloc((KC,), F32)
    sc.dma("sp", d_misc, tmpf, identf, writes=[B_ident])
    sc.dma("sp", d_misc, cT_sb, cT, writes=[B_ident])
    sc.dma("sp", d_misc, g1n_sb, g1n, writes=[B_ident])
    sc.dma("sp", d_misc, g2n_sb, g2n, writes=[B_ident])
    sc.dma("sp", d_misc, psc_sb, pscale, writes=[B_ident])
    sc.op("dve", lambda e: e.tensor_copy(out=ident, in_=tmpf), reads=[B_ident], writes=[B_small])
    sc.op("dve", lambda e: e.memset(epsb, EPS), writes=[B_small])
    sc.op("act", lambda e: e.activation(out=cact, in_=cT_sb, func=AF.Silu), reads=[B_ident], writes=[B_cact])

    brow = [ar.alloc((256,), F32, parts=1) for _ in range(2)]
    B_brow = [Buf("brow0"), Buf("brow1")]
    d_brow = [sc.dsem("dbrow0"), sc.dsem("dbrow1")]
    mrow = [ar.alloc((256,), F32, parts=1) for _ in range(2)]
    B_mrow = [Buf("mrow0"), Buf("mrow1")]
    mctr = [0]

    def mod_gen(v, bank):
        for cb in range(8):
            mod_slab(v, (bank, 13 - bank)[cb % 2], cb)
            yield

    def mod_vector(v, bank):
        for cb in range(8):
            mod_slab(v, bank, cb)

    def mod_slab(v, bank, cb):
        if True:
            c0 = v * D + cb * 256
            slab, bs = load_w(w_ada[:, c0:c0 + 256].rearrange("(kc p) n -> p kc n", p=128), 4096, v3(256))
            if v >= 2:
                precast(1)
            j = mctr[0] % 2
            mctr[0] += 1
            sc.dma("sp", d_brow[j], brow[j], b_ada[0:1, c0:c0 + 256], writes=[B_brow[j]])

            def fn(e, slab=slab):
                ins = None
                for kc in range(KC):
                    ins = e.matmul(PS[bank][0:1, 0:256], lhsT=cact[:, kc:kc + 1], rhs=slab[:, kc, :],
                                   start=(kc == 0), stop=(kc == KC - 1))
                return ins
            sc.op("pe", fn, reads=[bs, B_cact], writes=[PB[bank]])
            sc.op("dve", lambda e, j=j: e.tensor_tensor(out=mrow[j], in0=PS[bank][0:1, 0:256], in1=brow[j], op=ALU.add),
                  reads=[B_brow[j]], writes=[PB[bank], B_mrow[j]])
            sc.dma("sp", d_mod[v], modrows[v:v + 1, cb * 256:(cb + 1) * 256], mrow[j], reads=[B_mrow[j]], writes=[B_mod[v]])

    def load_pp(dst, v, buf):
        sc.dma("sp", d_misc2, dst, modrows[v].rearrange("(kc p) -> p kc", p=128), reads=[B_mod[v]], writes=[buf],
               allow_slow_non_contiguous=True)

    off_xh = ar.off
    xh_all = ar.alloc((3, D), BF16)
    xh = [xh_all[:, i_, :] for i_ in range(3)]
    B_xh = [Buf("xh0"), Buf("xh1"), Buf("xh2")]
    junk = ar.alloc((D,), BF16)
    B_junk = Buf("junk")
    stat = ar.alloc((64,), F32)
    B_stats = [Buf(f"stat{i}") for i in range(16)]
    statc = [0]
    m_persist = ar.mark()

    hT = ar.alloc((KC, S), BF16)
    B_hTa = [Buf(f"hTa{m}") for m in range(NCH)]
    B_hTb = [Buf(f"hTb{m}") for m in range(NCH)]
    B_hT = [None] * NCH
    m_p1 = ar.mark()
    NXT = 6
    xt = [ar.alloc((D,), F32) for _ in range(NXT)]
    B_xt = [Buf(f"xt{i}") for i in range(NXT)]
    d_xt = [sc.dsem(f"dxt{i}") for i in range(NXT)]
    for t_ in range(4):
        wslot.append(ar.alloc((4096,), BF16))
        B_ws.append(Buf(f"ws_tmp{t_}"))
        d_ws.append(sc.dsem(f"dws_tmp{t_}"))
    PT = [PS[4][:, :].bitcast(BF16), PS[5][:, :].bitcast(BF16)]

    def rms_rstd(src_ap, src_bufs, width, excl=(), jbuf=None):
        jb = junk if jbuf is None else jbuf
        slot = statc[0] % 16
        statc[0] += 1
        col = 4 * slot
        Bs = B_stats[slot]
        ss = stat[:, col:col + 1]
        sd = stat[:, col + 1:col + 2]
        rs = stat[:, col + 2:col + 3]
        sc.op("act", lambda e: e.activation(out=jb[:, 0:width], in_=src_ap, func=AF.Square, accum_out=ss),
              reads=src_bufs, writes=[B_junk, Bs] + list(excl))
        sc.op("act", lambda e: e.activation(out=sd, in_=ss, func=AF.Sqrt, scale=1.0 / width, bias=epsb),
              reads=[B_small], writes=[Bs])
        sc.op("dve", lambda e: e.reciprocal(out=rs, in_=sd), writes=[Bs])
        return rs, Bs

    def nt_A(src_ap, src_buf, it):
        rs, Bs = rms_rstd(src_ap, [src_buf], D)
        xhb = xh[it % 3]
        Bx = B_xh[it % 3]
        sc.op("dve", lambda e: e.tensor_scalar(out=xhb, in0=src_ap, scalar1=rs, scalar2=None, op0=ALU.mult),
              reads=[src_buf, Bs], writes=[Bx])
        return xhb, Bx

    def nt_B(xhb, Bx, dstT, dst_bufs, m_local):
        for half in range(2):
            bank = 4 + half

            def fn(e, half=half):
                ins = None
                for j in range(8):
                    kc = half * 8 + j
                    ins = e.transpose(out=PT[half][:, j * 128:(j + 1) * 128], in_=xhb[:, kc * 128:(kc + 1) * 128], identity=ident)
                return ins
            sc.op("pe", fn, reads=[Bx, B_small], writes=[PB[bank]])
            o = dstT[:, half * 8:(half + 1) * 8, m_local * 128:(m_local + 1) * 128]
            i_ = PT[half].rearrange("p (a b) -> p a b", b=128)
            if half == 0:
                sc.op("act", lambda e, o=o, i_=i_: e.activation(out=o, in_=i_, func=AF.Copy), writes=[PB[bank], dst_bufs[half]])
            else:
                sc.op("dve", lambda e, o=o, i_=i_: e.tensor_copy(out=o, in_=i_), writes=[PB[bank], dst_bufs[half]])

    def nt_gen(src_ap, src_buf, dstT, dst_bufs, m_local, it, pre=None):
        if pre is not None:
            pre()
        xhb, Bx = nt_A(src_ap, src_buf, it)
        yield
        nt_B(xhb, Bx, dstT, dst_bufs, m_local)

    def modulate(dstT, a_t, b_t, ab_buf, bufs_lo, bufs_hi, kc_bufs, engines):
        for kc in range(KC):
            eng = engines[kc % len(engines)]
            bufs = bufs_lo if kc < 8 else bufs_hi
            t_ = dstT[:, kc, :]
            if eng == "act":
                sc.op("act", lambda e, t_=t_, kc=kc: e.activation(out=t_, in_=t_, func=AF.Identity,
                                                                 scale=a_t[:, kc:kc + 1], bias=b_t[:, kc:kc + 1]),
                      reads=[ab_buf] + list(bufs), writes=[kc_bufs[kc]])
            else:
                sc.op(eng, lambda e, t_=t_, kc=kc: e.tensor_scalar(out=t_, in0=t_, scalar1=a_t[:, kc:kc + 1], scalar2=b_t[:, kc:kc + 1],
                                                                  op0=ALU.mult, op1=ALU.add),
                      reads=[ab_buf] + list(bufs), writes=[kc_bufs[kc]])

    def xload(m):
        i = m % NXT
        sc.dma("sp", d_xt[i], xt[i], x[m * 128:(m + 1) * 128, :], writes=[B_xt[i]])

    def p1_gen(m):
        i = m % NXT
        return nt_gen(xt[i], B_xt[i], hT, [B_hTa[m], B_hTb[m]], m, m)
    XPF = 3
    for m in range(XPF):
        xload(m)
    rounds = pipeline_rounds([p1_gen(m) for m in range(NCH)], 3)
    mods01 = [mod_gen(0, 7), mod_gen(1, 7)]
    sc_pp = ar.alloc((KC,), F32)
    B_pp = Buf("pp")
    for i in range(max(NCH + 4, 16)):
        if i + XPF < NCH:
            xload(i + XPF)
        if i < 16:
            next(mods01[i // 8])
        if i == 8:
            load_pp(b1, 0, B_pp)
        next(rounds, None)
    for _ in rounds:
        pass
    load_pp(sc_pp, 1, B_pp)
    B_ab1 = Buf("ab1")
    sc.op("dve", lambda e: e.scalar_tensor_tensor(out=a1, in0=sc_pp, scalar=1.0, in1=g1n_sb, op0=ALU.add, op1=ALU.mult),
          reads=[B_pp, B_ident], writes=[B_ab1])
    B_hTk = [Buf(f"hTk{kc}") for kc in range(KC)]
    modulate(hT, a1, b1, B_ab1, B_hTa, B_hTb, B_hTk, ["dve", "dve", "act"])

    if debug_stop in (11, 12, 13):
        sc.barrier()
        sc.replay()
        return nc
    if debug_stop == 1:
        dbg = nc.dram_tensor("dbg_hT", [128, KC, S], BF16, kind="ExternalOutput").ap()
        for kc in range(KC):
            sc.dma("sp", d_misc, dbg[:, kc, :], hT[:, kc, :], reads=B_hTk)
        sc.barrier()
        sc.replay()
        return nc
    del wslot[3:], B_ws[3:], d_ws[3:]
    ar.reset(m_p1)
    sc.barrier(skip=("pool",), skip_dsems=(d_pre,))
    B_fence2 = Buf("fence2")
    tok_f2 = sc.op("dve", lambda e: e.memset(stat[:, 3:4], 0.0), writes=[B_fence2])
    wslot.append(xh_all[:, 0:2, :].rearrange("p a b -> p (a b)"))
    B_ws.append(Buf("ws_alias"))
    B_ws[-1].w = tok_f2
    d_ws.append(sc.dsem("dws_alias"))
    wslot.append(ar.t[0:128, off_xh + 8192:off_xh + 16384].bitcast(BF16))
    B_ws.append(Buf("ws_alias2"))
    B_ws[-1].w = tok_f2
    d_ws.append(sc.dsem("dws_alias2"))

    m_p2 = ar.mark()
    upad = ar.alloc((2, 2, S + 16), F32)
    B_up = [[Buf(f"up{g_}{f_}") for f_ in range(2)] for g_ in range(2)]
    sa = ar.alloc((S + 16,), F32)
    sb = ar.alloc((S + 16,), F32)
    B_sa, B_sb = Buf("sa"), Buf("sb")
    invc_sb = [ar.alloc((S,), F32) for _ in range(2)]
    B_invc = [Buf("invc0"), Buf("invc1")]
    d_invc = [sc.dsem("dinvc0"), sc.dsem("dinvc1")]
    pooledT = ar.alloc((2, 2, S), BF16)
    B_pooled = [[Buf(f"pooled{g_}{f_}") for f_ in range(2)] for g_ in range(2)]
    stg = [ar.alloc((S,), BF16) for _ in range(2)]
    B_stg = [Buf("stg0"), Buf("stg1")]
    d_stg = [sc.dsem("dstg0"), sc.dsem("dstg1")]
    pw = ar.alloc((4, 2, 256), BF16)
    B_pw = Buf("pw")
    d_pw = sc.dsem("dpw")
    sc.op("dve", lambda e: e.memset(upad[:, :, :, 0:8], 0.0), writes=[b_ for bb in B_up for b_ in bb])
    sc.op("dve", lambda e: e.memset(upad[:, :, :, 8 + S:16 + S], 0.0), writes=[b_ for bb in B_up for b_ in bb])
    abank = [0]
    stgc = [0]

    def inproj_group(slab, bs, fc, tt):
        bank = abank[0] % 4
        abank[0] += 1

        def fn(e):
            ins = None
            for kc in range(KC):
                ins = e.matmul(PS[bank][:, :], lhsT=slab[:, kc, fc * 128:(fc + 1) * 128], rhs=hT[:, kc, tt * 512:(tt + 1) * 512],
                               start=(kc == 0), stop=(kc == KC - 1))
            return ins
        sc.op("pe", fn, reads=[bs] + B_hTk, writes=[PB[bank]])
        return bank

    def w_in_slab(c0):
        r_ = load_w(w_in[:, c0:c0 + 256].rearrange("(kc p) n -> p kc n", p=128), 4096, v3(256))
        precast(1)
        return r_

    def pool_gen(gi):
        g2 = gi % 2
        L = S + 16
        slab, bs = w_in_slab(gi * 256)
        if gi == 1:
            for g_ in range(4):
                sc.dma("pool", d_pw, pw[:, g_], pool_w[g_].rearrange("(cc p) d -> p cc d", p=128), reads=[B_fence2], writes=[B_pw])
        sc.dma("sp", d_invc[g2], invc_sb[g2], invc[gi], writes=[B_invc[g2]])
        for fc in range(2):
            for tt in range(4):
                bank = inproj_group(slab, bs, fc, tt)
                sc.op("act", lambda e, bank=bank, fc=fc, tt=tt: e.activation(out=upad[:, g2, fc, 8 + tt * 512:8 + (tt + 1) * 512],
                                                                          in_=PS[bank][:, :], func=AF.Copy),
                      writes=[PB[bank], B_up[g2][fc]])
        yield
        for fc in range(2):
            up = upad[:, g2, fc, :]
            sc.op("dve", lambda e, up=up: e.tensor_tensor(out=sa[:, 1:L], in0=up[:, 0:L - 1], in1=up[:, 1:L], op=ALU.add),
                  reads=[B_up[g2][fc]], writes=[B_sa])
            cur, curB, oth, othB, lo = sa, B_sa, sb, B_sb, 1
            for sh in (1, 2, 4)[:gi]:
                nlo = lo + sh
                hi = L - nlo

                def f(e, cur=cur, oth=oth, nlo=nlo, hi=hi, sh=sh):
                    return e.tensor_tensor(out=oth[:, nlo:hi], in0=cur[:, nlo - sh:hi - sh], in1=cur[:, nlo + sh:hi + sh], op=ALU.add)
                sc.op("dve", f, reads=[curB], writes=[othB])
                cur, curB, oth, othB, lo = oth, othB, cur, curB, nlo
            sc.op("dve", lambda e, cur=cur, oth=oth: e.tensor_tensor(out=oth[:, 8:8 + S], in0=cur[:, 8:8 + S], in1=invc_sb[g2], op=ALU.mult),
                  reads=[curB, B_invc[g2]], writes=[othB])
            sc.op("dve", lambda e, oth=oth, up=up, fc=fc: e.tensor_tensor(out=pooledT[:, g2, fc, :], in0=oth[:, 8:8 + S], in1=up[:, 8:8 + S], op=ALU.subtract),
                  reads=[othB, B_up[g2][fc]], writes=[B_pooled[g2][fc]])
        yield
        for dc in range(2):
            si = stgc[0] % 2
            stgc[0] += 1
            for tt in range(4):
                bank = abank[0] % 4
                abank[0] += 1

                def fn(e, bank=bank, dc=dc, tt=tt):
                    ins = None
                    for cc in range(2):
                        ins = e.matmul(PS[bank][:, :], lhsT=pw[:, gi, cc, dc * 128:(dc + 1) * 128],
                                       rhs=pooledT[:, g2, cc, tt * 512:(tt + 1) * 512], start=(cc == 0), stop=(cc == 1))
                    return ins
                sc.op("pe", fn, reads=[B_pw] + B_pooled[g2], writes=[PB[bank]])
                col = gi * 2 + dc
                sc.op("act", lambda e, bank=bank, si=si, tt=tt, col=col: e.activation(out=stg[si][:, tt * 512:(tt + 1) * 512], in_=PS[bank][:, :],
                                                                                 func=AF.Copy, scale=psc_sb[:, col:col + 1]),
                      reads=[B_ident], writes=[PB[bank], B_stg[si]])
            sc.dma("sp", d_stg[si], catT[gi * 2 + dc], stg[si], reads=[B_stg[si]], writes=[B_cat[gi * 2 + dc]])

    run_pipeline([pool_gen(gi) for gi in range(4)], 3)

    if debug_stop == 2:
        sc.barrier()
        sc.replay()
        return nc
    ar.reset(m_p2)
    sc.barrier(skip=("pool",), skip_dsems=(d_pre,))

    junk3 = ar.alloc((DH,), BF16)
    cm = ar.alloc((514,), F32)
    dec_sb = ar.alloc((8,), F32)
    lg = ar.alloc((8,), F32)
    cdec = ar.alloc((8,), F32)
    kd = ar.alloc((8,), F32)
    DmT = ar.alloc((4, 128), F32)
    rowf = ar.alloc((4, 128), F32)
    rowb = ar.alloc((4, 128), F32)
    ttmp = ar.alloc((128,), F32)
    B_tab = Buf("tab")
    B_tt = Buf("ttmp")
    d_c = sc.dsem("dconst")
    sc.dma("sp", d_c, cm, cmat, writes=[B_tab])
    sc.dma("sp", d_c, dec_sb, decay, writes=[B_tab])
    sc.op("act", lambda e: e.activation(out=lg, in_=dec_sb, func=AF.Exp), reads=[B_tab], writes=[B_tt])
    sc.op("dve", lambda e: e.tensor_scalar(out=lg, in0=lg, scalar1=-1.0, scalar2=None, op0=ALU.mult), reads=[B_tt], writes=[B_tt])
    sc.op("act", lambda e: e.activation(out=cdec, in_=lg, func=AF.Exp, scale=128.0), reads=[B_tt], writes=[B_tab])
    Am, Bm, rF, rB = cm[:, 0:128], cm[:, 128:256], cm[:, 256:384], cm[:, 384:512]
    cF, cB = cm[:, 512:513], cm[:, 513:514]
    for h in range(NH):
        lf = lg[:, h:h + 1]
        lb = lg[:, 4 + h:5 + h]
        sc.op("dve", lambda e, lf=lf: e.tensor_scalar(out=ttmp, in0=Am, scalar1=lf, scalar2=LN16, op0=ALU.mult, op1=ALU.add),
              reads=[B_tab, B_tt], writes=[B_tt])
        sc.op("dve", lambda e, lb=lb: e.scalar_tensor_tensor(out=ttmp, in0=Bm, scalar=lb, in1=ttmp, op0=ALU.mult, op1=ALU.add),
              reads=[B_tab, B_tt], writes=[B_tt])
        sc.op("act", lambda e, h=h: e.activation(out=DmT[:, h, :], in_=ttmp, func=AF.Exp), reads=[B_tt], writes=[B_tt])
        sc.op("dve", lambda e, lf=lf: e.tensor_scalar(out=ttmp, in0=rF, scalar1=lf, scalar2=LN16, op0=ALU.mult, op1=ALU.add),
              reads=[B_tab, B_tt], writes=[B_tt])
        sc.op("act", lambda e, h=h: e.activation(out=rowf[:, h, :], in_=ttmp, func=AF.Exp), reads=[B_tt], writes=[B_tt])
        sc.op("dve", lambda e, lb=lb: e.tensor_scalar(out=ttmp, in0=rB, scalar1=lb, scalar2=LN16, op0=ALU.mult, op1=ALU.add),
              reads=[B_tab, B_tt], writes=[B_tt])
        sc.op("act", lambda e, h=h: e.activation(out=rowb[:, h, :], in_=ttmp, func=AF.Exp), reads=[B_tt], writes=[B_tt])
        sc.op("dve", lambda e, lf=lf: e.tensor_tensor(out=ttmp[:, 0:1], in0=cF, in1=lf, op=ALU.mult), reads=[B_tab, B_tt], writes=[B_tt])
        sc.op("act", lambda e, h=h: e.activation(out=kd[:, h:h + 1], in_=ttmp[:, 0:1], func=AF.Exp), reads=[B_tt], writes=[B_tt])
        sc.op("dve", lambda e, lb=lb: e.tensor_tensor(out=ttmp[:, 0:1], in0=cB, in1=lb, op=ALU.mult), reads=[B_tab, B_tt], writes=[B_tt])
        sc.op("act", lambda e, h=h: e.activation(out=kd[:, 4 + h:5 + h], in_=ttmp[:, 0:1], func=AF.Exp), reads=[B_tt], writes=[B_tt])
    B_T = B_tt

    qT = ar.alloc((2, S), BF16)
    kT = ar.alloc((2, S), BF16)
    vtok = ar.alloc((NCH, DH), BF16)
    sgT = ar.alloc((2, S), BF16)
    Sb_all = ar.alloc((NCH, 512), BF16)
    Sf_all = ar.alloc((NCH, 512), BF16)
    rstg = [ar.alloc((2, 128), BF16) for _ in range(4)]
    B_qT = [Buf(f"qT{t}") for t in range(4)]
    B_kT = [Buf(f"kT{t}") for t in range(4)]
    B_v = [Buf(f"v{n}") for n in range(NCH)]
    B_sg = [Buf(f"sg{t}") for t in range(4)]
    B_Sb = [Buf(f"Sb{n}") for n in range(NCH)]
    B_Sf = [Buf(f"Sf{n}") for n in range(NCH)]
    B_rstg = [Buf(f"rstg{n}") for n in range(4)]
    d_rstg = [sc.dsem(f"drstg{n}") for n in range(4)]
    csb = [ar.alloc((2, 256), F32) for _ in range(2)]
    B_cs = [Buf("cs0"), Buf("cs1")]
    d_cs = [sc.dsem("dcs0"), sc.dsem("dcs1")]
    rt = [ar.alloc((512,), F32) for _ in range(2)]
    B_rt = [Buf(f"rt{i}") for i in range(2)]
    S32 = {d_: [ar.alloc((512,), F32) for _ in range(2)] for d_ in "fb"}
    B_S32 = {d_: [Buf(f"S32{d_}0"), Buf(f"S32{d_}1")] for d_ in "fb"}
    NKR = 4
    kdt = [ar.alloc((DH,), BF16) for _ in range(NKR)]
    B_kdt = [Buf(f"kdt{i}") for i in range(NKR)]
    PTm = [ar.alloc((128,), BF16) for _ in range(2)]
    B_PTm = [Buf("PTm0"), Buf("PTm1")]
    qf = [ar.alloc((2, 128), BF16) for _ in range(2)]
    qb = [ar.alloc((2, 128), BF16) for _ in range(2)]
    B_qf = [Buf("qf0"), Buf("qf1")]
    B_qb = [Buf("qb0"), Buf("qb1")]
    retn = [ar.alloc((DH,), BF16) for _ in range(2)]
    B_retn = [Buf("retn0"), Buf("retn1")]
    PTb7 = PS[7][:, :].bitcast(BF16)
    PTb3 = PS[3][:, :].bitcast(BF16)
    csc = [0]
    kdc = [0]
    dlc = [0]
    sc.op("dve", lambda e: e.memset(Sb_all[:, NCH - 1, :], 0.0), writes=[B_Sb[NCH - 1]])
    sc.op("dve", lambda e: e.memset(Sf_all[:, 0, :], 0.0), writes=[B_Sf[0]])

    def rope_evac(dstT, dstB, tt, b0, b1):
        for hf in range(2):
            i = csc[0] % 2
            csc[0] += 1
            c0 = tt * 512 + hf * 256
            sc.dma("sp", d_cs[i], csb[i], cs[:, :, c0:c0 + 256], writes=[B_cs[i]])
            co, si = csb[i][:, 0, :], csb[i][:, 1, :]
            p0, p1 = PS[b0][:, hf * 256:(hf + 1) * 256], PS[b1][:, hf * 256:(hf + 1) * 256]
            sl = slice(c0, c0 + 256)
            r0, r1 = rt[0][:, 0:256], rt[1][:, 0:256]

            def ops(i=i, co=co, si=si, p0=p0, p1=p1, sl=sl, r0=r0, r1=r1):
                sc.op("dve", lambda e: e.tensor_tensor(out=r0, in0=p0, in1=co, op=ALU.mult), reads=[B_cs[i]], writes=[PB[b0], B_rt[0]])
                sc.op("dve", lambda e: e.tensor_tensor(out=r1, in0=p1, in1=si, op=ALU.mult), reads=[B_cs[i]], writes=[PB[b1], B_rt[1]])
                sc.op("dve", lambda e: e.tensor_tensor(out=dstT[:, 0, sl], in0=r0, in1=r1, op=ALU.subtract),
                      reads=[B_rt[0], B_rt[1]], writes=[dstB[tt]])
                sc.op("dve", lambda e: e.tensor_tensor(out=r0, in0=p0, in1=si, op=ALU.mult), reads=[B_cs[i]], writes=[PB[b0], B_rt[0]])
                sc.op("dve", lambda e: e.tensor_tensor(out=r1, in0=p1, in1=co, op=ALU.mult), reads=[B_cs[i]], writes=[PB[b1], B_rt[1]])
                sc.op("dve", lambda e: e.tensor_tensor(out=dstT[:, 1, sl], in0=r0, in1=r1, op=ALU.add),
                      reads=[B_rt[0], B_rt[1]], writes=[dstB[tt]])
            ops()

    def k_tok(n, col):
        j = kdc[0] % NKR
        tb, ptb = (7, PTb7) if kdc[0] % 2 == 0 else (3, PTb3)
        kdc[0] += 1

        def fn(e):
            ins = None
            for fc in range(2):
                ins = e.transpose(out=ptb[:, fc * 128:(fc + 1) * 128], in_=kT[:, fc, n * 128:(n + 1) * 128], identity=ident)
            return ins
        sc.op("pe", fn, reads=[B_kT[n // 4], B_small], writes=[PB[tb]])
        sc.op("act", lambda e: e.activation(out=kdt[j], in_=ptb[:, 0:256], func=AF.Copy, scale=kd[:, col:col + 1]),
              reads=[B_T], writes=[PB[tb], B_kdt[j]])
        return kdt[j], B_kdt[j]

    def state_delta(kt, kB, n):
        bank = 6 if dlc[0] % 2 == 0 else 2
        dlc[0] += 1

        def fn(e):
            ins = None
            for dc in range(2):
                ins = e.matmul(PS[bank][:, dc * 256:(dc + 1) * 256], lhsT=kt[:, dc * 128:(dc + 1) * 128], rhs=vtok[:, n, :],
                               start=True, stop=True)
            return ins
        sc.op("pe", fn, reads=[kB, B_v[n]], writes=[PB[bank]])
        return bank

    def chain_gen(h, dirn, n, st):
        col = h if dirn == "f" else 4 + h
        kt, kB = k_tok(n, col)
        yield
        bank = state_delta(kt, kB, n)
        yield
        cur = st[dirn]
        nxt = 1 - cur
        st[dirn] = nxt
        sc.op("dve", lambda e: e.scalar_tensor_tensor(out=S32[dirn][nxt], in0=S32[dirn][cur], scalar=cdec[:, col:col + 1], in1=PS[bank][:, :],
                                                     op0=ALU.mult, op1=ALU.add),
              reads=[B_S32[dirn][cur], B_tab], writes=[PB[bank], B_S32[dirn][nxt]])
        if dirn == "f":
            sc.op("act", lambda e: e.activation(out=Sf_all[:, n + 1, :], in_=S32[dirn][nxt], func=AF.Copy),
                  reads=[B_S32[dirn][nxt]], writes=[B_Sf[n + 1]])
        else:
            sc.op("act", lambda e: e.activation(out=Sb_all[:, n - 1, :], in_=S32[dirn][nxt], func=AF.Copy),
                  reads=[B_S32[dirn][nxt]], writes=[B_Sb[n - 1]])

    def out_gen(h, n):
        p = n % 2
        ns = slice(n * 128, (n + 1) * 128)
        tq = n // 4
        sbank = (4, 0)[p]
        obank = (5, 1)[p]
        tbank, ptb = ((7, PTb7), (3, PTb3))[p]

        def fsc(e):
            ins = None
            for dc in range(2):
                ins = e.matmul(PS[sbank][:, 0:128], lhsT=kT[:, dc, ns], rhs=qT[:, dc, ns], start=(dc == 0), stop=(dc == 1))
            return ins
        sc.op("pe", fsc, reads=[B_kT[tq], B_qT[tq]], writes=[PB[sbank]])
        sc.op("dve", lambda e: e.tensor_tensor(out=PTm[p], in0=PS[sbank][:, 0:128], in1=DmT[:, h, :], op=ALU.mult),
              reads=[B_T], writes=[PB[sbank], B_PTm[p]])
        sc.op("dve", lambda e: e.tensor_tensor(out=qf[p], in0=qT[:, :, ns], in1=rowf[:, h:h + 1, :].to_broadcast([128, 2, 128]), op=ALU.mult),
              reads=[B_qT[tq], B_T], writes=[B_qf[p]])
        sc.op("dve", lambda e: e.tensor_tensor(out=qb[p], in0=qT[:, :, ns], in1=rowb[:, h:h + 1, :].to_broadcast([128, 2, 128]), op=ALU.mult),
              reads=[B_qT[tq], B_T], writes=[B_qb[p]])
        yield

        def fo(e):
            e.matmul(PS[obank][:, 0:256], lhsT=PTm[p], rhs=vtok[:, n, :], start=True, stop=False)
            for dc in range(2):
                e.matmul(PS[obank][:, 0:256], lhsT=qf[p][:, dc, :], rhs=Sf_all[:, n, dc * 256:(dc + 1) * 256], start=False, stop=False)
            ins = None
            for dc in range(2):
                ins = e.matmul(PS[obank][:, 0:256], lhsT=qb[p][:, dc, :], rhs=Sb_all[:, n, dc * 256:(dc + 1) * 256], start=False, stop=(dc == 1))
            return ins
        sc.op("pe", fo, reads=[B_PTm[p], B_v[n], B_qf[p], B_qb[p], B_Sf[n], B_Sb[n]], writes=[PB[obank]])
        yield
        rs, Bs = rms_rstd(PS[obank][:, 0:256], [], DH, excl=[PB[obank]], jbuf=junk3)
        sc.op("act", lambda e: e.activation(out=retn[p], in_=PS[obank][:, 0:256], func=AF.Copy, scale=rs),
              reads=[Bs], writes=[PB[obank], B_retn[p]])
        yield

        def ftr(e):
            ins = None
            for ec in range(2):
                ins = e.transpose(out=ptb[:, 512 + ec * 128:512 + (ec + 1) * 128], in_=retn[p][:, ec * 128:(ec + 1) * 128], identity=ident)
            return ins
        sc.op("pe", ftr, reads=[B_retn[p], B_small], writes=[PB[tbank]])
        r4 = n % 4
        sc.op("dve", lambda e: e.tensor_tensor(out=rstg[r4], in0=ptb[:, 512:768].rearrange("p (a b) -> p a b", b=128),
                                              in1=sgT[:, :, ns], op=ALU.mult),
              reads=[B_sg[tq]], writes=[PB[tbank], B_rstg[r4]])
        sc.dma("sp", d_rstg[r4], catT[8 + 2 * h:10 + 2 * h, :, ns].rearrange("c p t -> p c t"), rstg[r4],
               reads=[B_rstg[r4]], writes=[B_cat[8 + 2 * h], B_cat[9 + 2 * h]])

    B_pp2 = Buf("pp2")
    B_ab2 = Buf("ab2")
    for h in range(NH):
        if h == 2:
            load_pp(b2, 3, B_pp2)
        if h == 3:
            load_pp(sc2_pp, 4, B_pp2)
            sc.op("dve", lambda e: e.scalar_tensor_tensor(out=a2, in0=sc2_pp, scalar=1.0, in1=g2n_sb, op0=ALU.add, op1=ALU.mult),
                  reads=[B_pp2, B_ident], writes=[B_ab2])
        mg = mod_gen(2 + h, 6)
        for which, dstT, dstB in ((0, qT, B_qT), (1, kT, B_kT)):
            slab, bs = w_in_slab(1024 + which * 1024 + h * 256)
            for tt in range(4):
                b0 = inproj_group(slab, bs, 0, tt)
                b1_ = inproj_group(slab, bs, 1, tt)
                rope_evac(dstT, dstB, tt, b0, b1_)
            next(mg)
            next(mg)
        slab, bs = w_in_slab(3072 + h * 256)
        for n2 in range(NCH // 2):
            bank = abank[0] % 4
            abank[0] += 1

            def fn(e, slab=slab, n2=n2, bank=bank):
                ins = None
                for j in range(2):
                    n = n2 * 2 + j
                    for kc in range(KC):
                        ins = e.matmul(PS[bank][:, j * 256:(j + 1) * 256], lhsT=hT[:, kc, n * 128:(n + 1) * 128], rhs=slab[:, kc, :],
                                       start=(kc == 0), stop=(kc == KC - 1))
                return ins
            sc.op("pe", fn, reads=[bs] + B_hTk, writes=[PB[bank]])
            sc.op("act", lambda e, bank=bank, n2=n2: e.activation(out=vtok[:, n2 * 2:n2 * 2 + 2, :],
                                                                 in_=PS[bank][:, :].rearrange("p (a b) -> p a b", b=256), func=AF.Copy),
                  writes=[PB[bank], B_v[n2 * 2], B_v[n2 * 2 + 1]])
        next(mg)
        next(mg)
        slab, bs = w_in_slab(4096 + h * 256)
        for fc in range(2):
            for tt in range(4):
                bank = inproj_group(slab, bs, fc, tt)
                sc.op("act", lambda e, bank=bank, fc=fc, tt=tt: e.activation(out=sgT[:, fc, tt * 512:(tt + 1) * 512], in_=PS[bank][:, :], func=AF.Silu),
                      writes=[PB[bank], B_sg[tt]])

        next(mg)
        next(mg)
        st = {"f": 0, "b": 0}
        sc.op("dve", lambda e: e.memset(S32["f"][0], 0.0), writes=[B_S32["f"][0]])
        sc.op("dve", lambda e: e.memset(S32["b"][0], 0.0), writes=[B_S32["b"][0]])
        gens = []
        for i in range(NCH - 1):
            gens.append(chain_gen(h, "f", i, st))
            gens.append(chain_gen(h, "b", NCH - 1 - i, st))
        run_pipeline(gens, 6)
        run_pipeline([out_gen(h, n) for n in range(NCH)], 4)

    peak3 = ar.off
    if debug_stop == 3:
        sc.barrier()
        sc.replay()
        return nc
    precast(len(pre_jobs))
    for b_ in B_wsc:
        b_.w = (d_pre.h, d_pre.val)
    ar.reset(m_persist)
    sc.barrier(skip=("pool",))
    B_fence4 = Buf("fence4")
    tok_f4 = sc.op("dve", lambda e: e.memset(stat[:, 7:8], 0.0), writes=[B_fence4])

    g1_bc = ar.alloc((D,), F32)
    g2_bc = ar.alloc((D,), F32)
    fg_bc = ar.alloc((D,), F32)
    B_bc = Buf("bc")
    B_bc2 = Buf("bc2")
    d_bc = sc.dsem("dbc")
    d_bc2 = sc.dsem("dbc2")
    x1 = ar.alloc((4, D), F32)
    B_x1 = [Buf(f"x1_{m}") for m in range(4)]
    d_x1 = [sc.dsem(f"dx1_{m}") for m in range(4)]
    d_o = [sc.dsem(f"dout{m}") for m in range(4)]
    ctile = ar.alloc((KC, 512), BF16)
    B_c = Buf("ctile")
    B_h2 = [[Buf(f"h2_{m}_{hf}") for hf in range(2)] for m in range(4)]
    B_h2k = [Buf(f"h2k{kc}") for kc in range(KC)]
    B_h2all = [b_ for bb in B_h2 for b_ in bb] + B_h2k
    d_ct = sc.dsem("dct")
    actT = ar.alloc((NFC, 512), BF16)
    B_act = [Buf(f"act{j}") for j in range(NFC)]
    sgt = [ar.alloc((512,), F32) for _ in range(2)]
    B_sgt = [Buf("sgt0"), Buf("sgt1")]
    tmpA = [ar.alloc((512,), F32) for _ in range(2)]
    B_tmpA = [Buf("tmpA0"), Buf("tmpA1")]
    while ar.off + 8192 + 64 <= ar.n:
        wslot.append(ar.alloc((4096,), BF16))
        B_ws.append(Buf(f"ws{len(B_ws)}"))
        B_ws[-1].w = tok_f4
        d_ws.append(sc.dsem(f"dws{len(d_ws)}"))
    ring4 = [0, 1, 2] + list(range(5, len(wslot)))
    slot_ctr[0] = 0

    def load_w4(src_ap, ncols_total, view, sid=None):
        i = ring4[slot_ctr[0] % len(ring4)]
        slot_ctr[0] += 1
        flat = wslot[i][:, 0:ncols_total]
        dst = view(flat)
        if sid is None:
            sc.dma("pool", d_ws[i], dst, src_ap, writes=[B_ws[i]])
        else:
            sc.dma("pool", d_ws[i], flat, wsc[sid][:, 0:ncols_total], reads=[B_wsc[sid]], writes=[B_ws[i]])
        return dst, B_ws[i]

    tctr = [0]
    def load_ctile(tt):
        sc.dma("sp", d_ct, ctile, catT[:, :, tt * 512:(tt + 1) * 512].rearrange("kc p t -> p kc t"), reads=B_cat, writes=[B_c] + B_h2all)
    load_ctile(0)
    for m in range(4):
        sc.dma("sp", d_x1[m], x1[:, m, :], x[m * 128:(m + 1) * 128, :], writes=[B_x1[m]])
    sc.dma("sp", d_bc, g1_bc, modrows[2].partition_broadcast(128), reads=[B_mod[2]], writes=[B_bc])
    sc.dma("sp", d_bc2, g2_bc, modrows[5].partition_broadcast(128), reads=[B_mod[5]], writes=[B_bc2])
    sc.dma("sp", d_bc2, fg_bc, fgr.partition_broadcast(128), writes=[B_bc2])
    for tt in range(4):
        for m in range(4):
            r0 = (tt * 4 + m) * 128
            if tt > 0:
                sc.dma("sp", d_x1[m], x1[:, m, :], x[r0:r0 + 128, :], writes=[B_x1[m]])
        ob = 0
        for db in range(4):
            ds_ = slice(db * 512, (db + 1) * 512)
            halves = []
            for hf in range(2):
                halves.append(load_w4(None, 4096, v3(512), sid=db * 2 + hf))
            for m in range(4):
                bank = ob % 8
                ob += 1

                def fn(e, halves=halves, m=m, bank=bank):
                    ins = None
                    for kc in range(KC):
                        slab = halves[kc // 8][0]
                        ins = e.matmul(PS[bank][:, :], lhsT=ctile[:, kc, m * 128:(m + 1) * 128], rhs=slab[:, kc % 8, :],
                                       start=(kc == 0), stop=(kc == KC - 1))
                    return ins
                sc.op("pe", fn, reads=[halves[0][1], halves[1][1], B_c], writes=[PB[bank]])
                ti = tctr[0] % 2
                tctr[0] += 1
                sc.op("dve", lambda e, bank=bank, ti=ti, ds_=ds_: e.tensor_tensor(out=tmpA[ti], in0=PS[bank][:, :], in1=g1_bc[:, ds_], op=ALU.mult),
                      reads=[B_bc], writes=[PB[bank], B_tmpA[ti]])
                sc.op("dve", lambda e, m=m, ti=ti, ds_=ds_: e.tensor_tensor(out=x1[:, m, ds_], in0=x1[:, m, ds_], in1=tmpA[ti], op=ALU.add),
                      reads=[B_tmpA[ti]], writes=[B_x1[m]])
        sc.op("dve", lambda e: e.memset(stat[:, 3:4], 0.0), writes=[B_c] + B_h2all)
        run_pipeline([nt_gen(x1[:, m, :], B_x1[m], ctile, B_h2[m], m, m) for m in range(4)], 2)
        modulate(ctile, a2, b2, B_ab2, [B_h2[m][0] for m in range(4)], [B_h2[m][1] for m in range(4)], B_h2k, ["dve", "act"])
        for jp in range(NFC // 2):
            cs_ = slice(jp * 256, (jp + 1) * 256)
            gsl, gB = load_w4(w_gate[:, cs_].rearrange("(kc p) n -> p kc n", p=128), 4096, v3(256))
            usl, uB = load_w4(w_up[:, cs_].rearrange("(kc p) n -> p kc n", p=128), 4096, v3(256))
            for jj in range(2):
                j = jp * 2 + jj
                gb, ub = 4 + j % 2, 6 + j % 2
                for slab, sB, bank in ((gsl, gB, gb), (usl, uB, ub)):
                    def fn(e, slab=slab, bank=bank, jj=jj):
                        ins = None
                        for kc in range(KC):
                            ins = e.matmul(PS[bank][:, :], lhsT=slab[:, kc, jj * 128:(jj + 1) * 128], rhs=ctile[:, kc, :],
                                           start=(kc == 0), stop=(kc == KC - 1))
                        return ins
                    sc.op("pe", fn, reads=[sB] + B_h2k, writes=[PB[bank]])
                si = j % 2
                sc.op("act", lambda e, si=si, gb=gb: e.activation(out=sgt[si], in_=PS[gb][:, :], func=AF.Silu),
                      writes=[PB[gb], B_sgt[si]])
                sc.op("dve", lambda e, si=si, ub=ub, j=j: e.tensor_tensor(out=actT[:, j, :], in0=PS[ub][:, :], in1=sgt[si], op=ALU.mult),
                      reads=[B_sgt[si]], writes=[PB[ub], B_act[j]])
        if tt < 3:
            load_ctile(tt + 1)
        for db in range(4):
            ds_ = slice(db * 512, (db + 1) * 512)
            bo = 4 * (db % 2)
            for jg in range(NFC // 4):
                slab, sB = load_w4(None, 2048, v3(512), sid=30 + db * 11 + jg)

                def fn(e, slab=slab, jg=jg, bo=bo):
                    ins = None
                    for a in range(4):
                        j = jg * 4 + a
                        for m in range(4):
                            ins = e.matmul(PS[bo + m][:, :], lhsT=actT[:, j, m * 128:(m + 1) * 128], rhs=slab[:, a, :],
                                           start=(j == 0), stop=(j == NFC - 1))
                    return ins
                sc.op("pe", fn, reads=[sB] + B_act[jg * 4:(jg + 1) * 4], writes=PB[bo:bo + 4])
            for m in range(4):
                ti = tctr[0] % 2
                tctr[0] += 1
                sc.op("dve", lambda e, m=m, ti=ti, ds_=ds_, bo=bo: e.tensor_tensor(out=tmpA[ti], in0=PS[bo + m][:, :], in1=g2_bc[:, ds_], op=ALU.mult),
                      reads=[B_bc2], writes=[PB[bo + m], B_tmpA[ti]])
                sc.op("dve", lambda e, m=m, ti=ti, ds_=ds_: e.tensor_tensor(out=x1[:, m, ds_], in0=x1[:, m, ds_], in1=tmpA[ti], op=ALU.add),
                      reads=[B_tmpA[ti]], writes=[B_x1[m]])
        for m in range(4):
            rs, Bs = rms_rstd(x1[:, m, :], [B_x1[m]], D)
            sc.op("dve", lambda e, m=m, rs=rs: e.scalar_tensor_tensor(out=x1[:, m, :], in0=x1[:, m, :], scalar=rs, in1=fg_bc, op0=ALU.mult, op1=ALU.mult),
                  reads=[Bs, B_bc2], writes=[B_x1[m]])
            r0 = (tt * 4 + m) * 128
            sc.dma("sp", d_o[m], out[r0:r0 + 128, :], x1[:, m, :], reads=[B_x1[m]])

    sc.barrier()
    sc.replay()
    print("arena peaks", peak3, ar.off, "ops", sc.cnt, "sems", len(sc.dsems) + 5)
    return nc


def _consts():
    half = 128
    inv = (1.0 / (np.float32(10000.0) ** np.linspace(0.0, 1.0, half, dtype=np.float32))).astype(np.float32)
    ang = (np.arange(S, dtype=np.float32)[:, None] * inv[None, :]).astype(np.float32)
    cs = np.stack([np.cos(ang).T, np.sin(ang).T], axis=1).astype(np.float32)
    j = np.arange(128)[:, None]
    i = np.arange(128)[None, :]
    cm = np.zeros((128, 514), np.float32)
    cm[:, 0:128] = np.maximum(i - j, 0)
    cm[:, 128:256] = np.maximum(j - i, 0)
    cm[:, 256:384] = (i + 1)
    cm[:, 384:512] = (128 - i)
    cm[:, 512] = 127 - np.arange(128)
    cm[:, 513] = np.arange(128)
    t = np.arange(S)
    invc = np.zeros((4, 128, S), np.float32)
    for gi, w in enumerate((2, 4, 8, 16)):
        lo = np.clip(t - w // 2, 0, S)
        hi = np.clip(t + w // 2, 0, S)
        invc[gi] = (1.0 / (hi - lo).astype(np.float32))[None, :]
    return cs, cm, invc, np.eye(128, dtype=np.float32)


_NC_CACHE = {}


def kernel(x, c, w_ada, b_ada, norm1_g, w_in, pool_w, pool_scale, ret_decay_fwd, ret_decay_bwd,
           w_out, norm2_g, w_gate, w_up, w_down, final_g):
    f = lambda a: np.ascontiguousarray(np.asarray(a, dtype=np.float32))
    x, c = f(x), f(c)
    cs, cm, invc, identf = _consts()
    pp = lambda v: np.ascontiguousarray(v.reshape(-1, 128).T)
    shared = {
        "w_ada": f(w_ada)[0], "b_ada": f(b_ada)[0].reshape(1, -1),
        "g1n": pp(f(norm1_g)[0]), "g2n": pp(f(norm2_g)[0]), "fg": f(final_g),
        "g1row": f(norm1_g)[0], "g2row": f(norm2_g)[0],
        "w_in": f(w_in)[0], "pool_w": f(pool_w)[0], "pscale": pp(f(pool_scale)[0]),
        "decay": np.ascontiguousarray(np.broadcast_to(
            np.concatenate([f(ret_decay_fwd)[0], f(ret_decay_bwd)[0]])[None, :], (128, 8))),
        "w_out": f(w_out)[0], "w_gate": f(w_gate)[0], "w_up": f(w_up)[0], "w_down": f(w_down)[0],
        "cs": cs, "cmat": cm, "invc": invc, "identf": identf,
    }
    if "nc" not in _NC_CACHE:
        _NC_CACHE["nc"] = build_nc()
    nc = _NC_CACHE["nc"]
    in_maps = []
    for b in range(NCORES):
        m = dict(shared)
        m["x"] = x[b]
        m["cT"] = pp(c[b])
        in_maps.append(m)
    res = run_bass_kernel_spmd(nc, in_maps, core_ids=list(range(NCORES)))
    return np.stack([np.asarray(r["out"], dtype=np.float32) for r in res.results], axis=0)
```
